# Optimizing a Trainium2 kernel written in Bass

```python
import math
import jax
import jax.numpy as jnp
from jax import lax
import numpy as np

D_MODEL = 2048
BATCH = 4
SEQ = 4096
DEPTH = 2

DIFF_HEADS = 8
DIFF_HEAD_DIM = 64
DIFF_WIDTH = DIFF_HEADS * 2 * DIFF_HEAD_DIM
PARTIAL_ROPE_DIM = DIFF_HEAD_DIM // 4
MLA_HEADS = 8
MLA_Q_RANK = 512
MLA_KV_RANK = 256
MLA_NOPE_DIM = 128
MLA_ROPE_DIM = 64
MLA_V_DIM = 128
MLA_WIDTH = MLA_HEADS * MLA_V_DIM
IN_SIZES = (DIFF_WIDTH, DIFF_WIDTH, DIFF_WIDTH, MLA_Q_RANK, MLA_KV_RANK, MLA_ROPE_DIM, D_MODEL, D_MODEL)
IN_COLS = 3 * DIFF_WIDTH + MLA_Q_RANK + MLA_KV_RANK + MLA_ROPE_DIM + 2 * D_MODEL
D_FF = -((-8 * D_MODEL) // (3 * 256)) * 256
PLE_DIM = 256
ROPE_THETA = 500000.0
Q_BLOCK = 128
NORM_EPS = 1e-6
SUBLN_EPS = 1e-5

kernel_name = 'hybrid_diffattn_mla_gated_encoder'


def rmsnorm(x, g, eps=NORM_EPS):
    xf = x.astype(jnp.float32)
    y = xf * lax.rsqrt(jnp.mean(xf * xf, axis=-1, keepdims=True) + eps)
    return (y * g.astype(jnp.float32)).astype(x.dtype)


def rope_tables(positions, dim):
    inv_freq = ROPE_THETA ** (-jnp.arange(0, dim, 2, dtype=jnp.float32) / dim)
    ang = positions.astype(jnp.float32)[..., None] * inv_freq
    return jnp.cos(ang), jnp.sin(ang)


def apply_rope(x, cos, sin):
    x1, x2 = jnp.split(x.astype(jnp.float32), 2, axis=-1)
    c = cos[:, :, None, :]
    s = sin[:, :, None, :]
    return jnp.concatenate([x1 * c - x2 * s, x2 * c + x1 * s], axis=-1).astype(x.dtype)


def apply_partial_rope(x, cos, sin):
    return jnp.concatenate([apply_rope(x[..., :PARTIAL_ROPE_DIM], cos, sin), x[..., PARTIAL_ROPE_DIM:]], axis=-1)


def to_blocks(t):
    b, s = t.shape[:2]
    return jnp.swapaxes(t.reshape((b, s // Q_BLOCK, Q_BLOCK) + t.shape[2:]), 0, 1)


def from_blocks(t):
    t = jnp.swapaxes(t, 0, 1)
    return t.reshape((t.shape[0], t.shape[1] * t.shape[2]) + t.shape[3:])


def diff_attention(q, k, v, lam):
    b, s = q.shape[:2]
    qh = q.reshape(b, s, DIFF_HEADS, 2, DIFF_HEAD_DIM)
    kh = k.reshape(b, s, DIFF_HEADS, 2, DIFF_HEAD_DIM)
    scale = DIFF_HEAD_DIM ** -0.5

    def one_block(qb):
        scores = jnp.einsum('bqhcd,bkhcd->bhcqk', qb, kh).astype(jnp.float32) * scale
        probs = jax.nn.softmax(scores, axis=-1)
        weights = probs[:, :, 0] - lam * probs[:, :, 1]
        return jnp.einsum('bhqk,bkhe->bqhe', weights.astype(v.dtype), v)

    return from_blocks(lax.map(one_block, to_blocks(qh)))


def latent_attention(q_nope, q_rope, k_nope, k_rope, v):
    scale = (MLA_NOPE_DIM + MLA_ROPE_DIM) ** -0.5

    def one_block(qs):
        qn, qr = qs
        scores = (jnp.einsum('bqhd,bkhd->bhqk', qn, k_nope).astype(jnp.float32)
                  + jnp.einsum('bqhr,bkr->bhqk', qr, k_rope).astype(jnp.float32)) * scale
        probs = jax.nn.softmax(scores, axis=-1)
        return jnp.einsum('bhqk,bkhd->bqhd', probs.astype(v.dtype), v)

    return from_blocks(lax.map(one_block, (to_blocks(q_nope), to_blocks(q_rope))))


def setup_inputs(seed: int = 0) -> dict:
    key = jax.random.key(seed)
    ks = iter(jax.random.split(key, 32))
    f32 = jnp.float32

    def w(shape, fan_in):
        return jax.random.normal(next(ks), shape, f32) * (fan_in ** -0.5)

    def gain(shape):
        return 1.0 + 0.02 * jax.random.normal(next(ks), shape, f32)

    x = jax.random.normal(next(ks), (BATCH, SEQ, D_MODEL), f32)
    p = jax.random.normal(next(ks), (DEPTH, BATCH, SEQ, PLE_DIM), f32)
    offsets = jax.random.randint(next(ks), (BATCH, 1), 0, SEQ, dtype=jnp.int32)
    positions = jnp.arange(SEQ, dtype=jnp.int32)[None, :] + offsets
    return {
        'x': x,
        'p': p,
        'positions': positions,
        'g_mix': gain((DEPTH, D_MODEL)),
        'w_in': w((DEPTH, D_MODEL, IN_COLS), D_MODEL),
        'lambda_q1': 0.1 * jax.random.normal(next(ks), (DEPTH, DIFF_HEAD_DIM), f32),
        'lambda_k1': 0.1 * jax.random.normal(next(ks), (DEPTH, DIFF_HEAD_DIM), f32),
        'lambda_q2': 0.1 * jax.random.normal(next(ks), (DEPTH, DIFF_HEAD_DIM), f32),
        'lambda_k2': 0.1 * jax.random.normal(next(ks), (DEPTH, DIFF_HEAD_DIM), f32),
        'g_subln': gain((DEPTH, 2 * DIFF_HEAD_DIM)),
        'g_q_latent': gain((DEPTH, MLA_Q_RANK)),
        'w_q_up': w((DEPTH, MLA_Q_RANK, MLA_HEADS * (MLA_NOPE_DIM + MLA_ROPE_DIM)), MLA_Q_RANK),
        'g_kv_latent': gain((DEPTH, MLA_KV_RANK)),
        'w_kv_up': w((DEPTH, MLA_KV_RANK, MLA_HEADS * (MLA_NOPE_DIM + MLA_V_DIM)), MLA_KV_RANK),
        'w_branch_diff': w((DEPTH, DIFF_WIDTH, D_MODEL), DIFF_WIDTH),
        'w_branch_mla': w((DEPTH, MLA_WIDTH, D_MODEL), MLA_WIDTH),
        'w_out': w((DEPTH, D_MODEL, D_MODEL), D_MODEL),
        'g_ffn': gain((DEPTH, D_MODEL)),
        'w_gate_up': w((DEPTH, D_MODEL, 2 * D_FF), D_MODEL),
        'w_down': w((DEPTH, D_FF, D_MODEL), D_FF),
        'w_ple_in': w((DEPTH, PLE_DIM, D_MODEL), PLE_DIM),
        'g_ple': gain((DEPTH, D_MODEL)),
        'w_ple_gate': w((DEPTH, D_MODEL, D_MODEL), D_MODEL),
        'g_final': gain((D_MODEL,)),
    }


def reference(x, p, positions, g_mix, w_in, lambda_q1, lambda_k1, lambda_q2, lambda_k2, g_subln,
              g_q_latent, w_q_up, g_kv_latent, w_kv_up, w_branch_diff, w_branch_mla, w_out,
              g_ffn, w_gate_up, w_down, w_ple_in, g_ple, w_ple_gate, g_final):
    f32 = jnp.float32
    b, s, _ = x.shape
    cos_p, sin_p = rope_tables(positions, PARTIAL_ROPE_DIM)
    cos_m, sin_m = rope_tables(positions, MLA_ROPE_DIM)
    split_at = [int(c) for c in np.cumsum(IN_SIZES)[:-1]]
    for i in range(DEPTH):
        lam_init = 0.8 - 0.6 * math.exp(-0.3 * i)
        h = rmsnorm(x, g_mix[i])
        z = h @ w_in[i]
        dq, dk, dv, cq, ckv, kr, ga, gb = jnp.split(z, split_at, axis=-1)

        dq = apply_partial_rope(dq.reshape(b, s, 2 * DIFF_HEADS, DIFF_HEAD_DIM), cos_p, sin_p)
        dk = apply_partial_rope(dk.reshape(b, s, 2 * DIFF_HEADS, DIFF_HEAD_DIM), cos_p, sin_p)
        dv = dv.reshape(b, s, DIFF_HEADS, 2 * DIFF_HEAD_DIM)
        lam = (jnp.exp(jnp.sum(lambda_q1[i].astype(f32) * lambda_k1[i].astype(f32)))
               - jnp.exp(jnp.sum(lambda_q2[i].astype(f32) * lambda_k2[i].astype(f32))) + lam_init)
        od = diff_attention(dq, dk, dv, lam)
        od = rmsnorm(od, g_subln[i], SUBLN_EPS) * (1.0 - lam_init)
        y_diff = od.reshape(b, s, DIFF_WIDTH) @ w_branch_diff[i]

        qf = (rmsnorm(cq, g_q_latent[i]) @ w_q_up[i]).reshape(b, s, MLA_HEADS, MLA_NOPE_DIM + MLA_ROPE_DIM)
        q_nope = qf[..., :MLA_NOPE_DIM]
        q_rope = apply_rope(qf[..., MLA_NOPE_DIM:], cos_m, sin_m)
        kvf = (rmsnorm(ckv, g_kv_latent[i]) @ w_kv_up[i]).reshape(b, s, MLA_HEADS, MLA_NOPE_DIM + MLA_V_DIM)
        k_nope = kvf[..., :MLA_NOPE_DIM]
        v_mla = kvf[..., MLA_NOPE_DIM:]
        k_rope = apply_rope(kr[:, :, None, :], cos_m, sin_m)[:, :, 0, :]
        om = latent_attention(q_nope, q_rope, k_nope, k_rope, v_mla)
        y_mla = om.reshape(b, s, MLA_WIDTH) @ w_branch_mla[i]

        merged = jax.nn.sigmoid(ga) * y_diff + jax.nn.sigmoid(gb) * y_mla
        x = x + merged @ w_out[i]

        gu = rmsnorm(x, g_ffn[i]) @ w_gate_up[i]
        gate, up = jnp.split(gu, 2, axis=-1)
        x = x + (jax.nn.silu(gate) * up) @ w_down[i]

        e = p[i] @ w_ple_in[i]
        x = x + jax.nn.sigmoid(rmsnorm(x, g_ple[i]) @ w_ple_gate[i]) * e
    return rmsnorm(x, g_final)
```

```python
import math
import os
from contextlib import ExitStack

import numpy as np
import concourse.bass as bass
import concourse.mybir as mybir
from concourse.bass_utils import run_bass_kernel_spmd

F32 = mybir.dt.float32
BF16 = mybir.dt.bfloat16
I32 = mybir.dt.int32
U8 = mybir.dt.uint8
AF = mybir.ActivationFunctionType
ALU = mybir.AluOpType
PI = float(np.pi)

D = 2048
T = 2048
S = 4096
NT = T // 128
NS = S // 128
KC = D // 128
INC = 8000
DFF = 5632
FC = DFF // 128
DEPTH = 2
C_DQ, C_DK, C_DV, C_CQ, C_CKV, C_KR, C_GA, C_GB = 0, 1024, 2048, 3072, 3584, 3840, 3904, 5952
ROPE_THETA = 500000.0
TB0 = int(os.environ.get("KTB0", "6"))


class Op:
    __slots__ = ("eng", "fn", "deps", "is_dma", "dkey", "idx", "sem", "val", "signal", "name")

    def __init__(self, eng, fn, is_dma=False, dkey=None, name=""):
        self.eng = eng
        self.fn = fn
        self.deps = []
        self.is_dma = is_dma
        self.dkey = dkey
        self.sem = None
        self.val = None
        self.signal = False
        self.name = name


class Plan:
    ENGS = ("pe", "act", "dve", "pool", "sp")

    def __init__(self):
        self.ops = []
        self.last_w = {}
        self.readers = {}
        self.last_eng = {}
        self.last_dma = {}
        self.bar = {}

    def add(self, eng, fn, reads=(), writes=(), is_dma=False, dkey=None, name="", extra_deps=()):
        op = Op(eng, fn, is_dma, dkey, name)
        op.idx = len(self.ops)
        deps = set()
        extra = list(extra_deps)
        if eng in self.bar:
            extra += self.bar.pop(eng)
        xr = []
        for b in reads:
            w = self.last_w.get(b)
            if w is not None:
                deps.add(w)
            if isinstance(b, str) and b.startswith("ps"):
                for r_ in self.readers.get(b, ()):
                    if r_.eng != eng:
                        xr.append(r_)
        extra += xr
        for b in writes:
            w = self.last_w.get(b)
            if w is not None:
                deps.add(w)
            for r in self.readers.get(b, ()):
                deps.add(r)
        extra = [d for d in extra if d is not None]
        for d in extra:
            deps.add(d)
        pruned = []
        for d in deps:
            if (not d.is_dma) and (not op.is_dma) and d.eng == op.eng:
                if d.eng == "pe":
                    continue
                raw = any(self.last_w.get(b) is d for b in reads)
                if not raw:
                    continue
            pruned.append(d)
        op.deps = sorted(pruned, key=lambda o: o.idx)
        for d in op.deps:
            d.signal = True
        for b in reads:
            self.readers.setdefault(b, []).append(op)
        for b in writes:
            self.last_w[b] = op
            self.readers[b] = []
        self.ops.append(op)
        if is_dma:
            self.last_dma[dkey] = op
        else:
            self.last_eng[eng] = op
        return op

    def dma(self, eng, fn, reads=(), writes=(), dkey=None, name="", extra_deps=()):
        assert dkey is not None
        return self.add(eng, fn, reads, writes, is_dma=True, dkey=dkey, name=name, extra_deps=extra_deps)

    def barrier(self):
        if os.environ.get("KMARK"):
            print("barrier at op", len(self.ops))
        deps = list(self.last_eng.values()) + list(self.last_dma.values())
        self.bar = {e: list(deps) for e in self.ENGS}

    def emit(self, nc, es, final_wait_ops=()):
        sems = {}

        def get_sem(key):
            if key not in sems:
                nm = "s_" + "_".join(str(k) for k in (key if isinstance(key, tuple) else (key,)))
                sems[key] = es.enter_context(nc.semaphore(nm))
            return sems[key]

        maxops = int(os.environ.get("KMAXOPS", "0"))
        if maxops:
            self.ops = self.ops[:maxops]
            final_wait_ops = [o for o in final_wait_ops if o.idx < maxops]
            for o in self.ops:
                o.signal = False
            for o in self.ops:
                for d_ in o.deps:
                    d_.signal = True
        counters = {}
        for op in self.ops:
            if op.is_dma:
                key = ("d", op.dkey)
                op.signal = True
                counters[key] = counters.get(key, 0) + 16
            else:
                if not op.signal:
                    continue
                key = ("e", op.eng)
                counters[key] = counters.get(key, 0) + 1
            op.sem = get_sem(key)
            op.val = counters[key]
        self.n_sems = len(sems)
        per_eng = {e: [] for e in self.ENGS}
        for op in self.ops:
            per_eng[op.eng].append(op)
        final_wait_ops = list(final_wait_ops)
        block = es.enter_context(nc.Block())

        def run(eng_name, eng):
            known = {}

            def waits(dl):
                need = {}
                for d in dl:
                    k = id(d.sem)
                    if known.get(k, 0) >= d.val:
                        continue
                    if k not in need or need[k][1] < d.val:
                        need[k] = (d.sem, d.val)
                for k, (s, v) in need.items():
                    eng.wait_ge(s, v)
                    known[k] = v

            for op in per_eng[eng_name]:
                waits(op.deps)
                ins = op.fn(eng)
                if op.signal:
                    ins.then_inc(op.sem, 16 if op.is_dma else 1)
            if eng_name == "sp":
                waits(final_wait_ops)

        @block.tensor
        def _(e):
            run("pe", e)

        @block.scalar
        def _(e):
            run("act", e)

        @block.vector
        def _(e):
            run("dve", e)

        @block.gpsimd
        def _(e):
            run("pool", e)

        @block.sync
        def _(e):
            run("sp", e)


ARENA_BYTES = 212000
OFF_A = 25 * 1024
OFF_B = 65 * 1024
OFF_C = 129 * 1024


class Builder:
    def __init__(self, layers, final_norm, dbg=()):
        self.layers = list(layers)
        self.final_norm = final_norm
        self.dbg = set(dbg)
        self.nc = bass.Bass("TRN2", target_bir_lowering=False)
        self.P = Plan()
        self.out_ops = []

    def dram(self, name, shape, dt, kind="Internal"):
        if kind == "Internal" and name in self.dbg:
            kind = "ExternalOutput"
        return self.nc.dram_tensor(name, list(shape), dt, kind=kind).ap()

    def carve(self, off, shape, dt):
        n = int(np.prod(shape[1:])) * mybir.dt.size(dt)
        off = (off + 31) // 32 * 32
        assert off + n <= ARENA_BYTES, (off, n)
        ap = self.arena[0:shape[0], off:off + n].bitcast(dt)
        if len(shape) == 3:
            ap = ap.rearrange("p (a b) -> p a b", b=shape[2])
        elif len(shape) == 4:
            ap = ap.rearrange("p (a b c) -> p a b c", b=shape[2], c=shape[3])
        return ap, off + n

    class Bump:
        def __init__(self, b, off, limit=ARENA_BYTES):
            self.b, self.off, self.limit = b, off, limit

        def __call__(self, shape, dt):
            ap, self.off = self.b.carve(self.off, shape, dt)
            assert self.off <= self.limit, (self.off, self.limit)
            return ap

    def mm(self, out, lhsT, rhs, start, stop, r, w):
        return self.P.add("pe", lambda e: e.matmul(out, lhsT=lhsT, rhs=rhs, start=start, stop=stop), r, w)

    def tr(self, out, in_, r, w):
        ident = self.ident[0:in_.shape[0], 0:in_.shape[0]]
        return self.P.add("pe", lambda e: e.transpose(out, in_, ident), r, w)

    def act(self, out, in_, func, r, w, **kw):
        return self.P.add("act", lambda e: e.activation(out=out, in_=in_, func=func, **kw), r, w)

    def op(self, eng, name, r, w, **kw):
        return self.P.add(eng, lambda e: getattr(e, name)(**kw), r, w)

    def ld(self, out, in_, r, w, dkey, eng="sp"):
        if len(out.shape) == 3 and out.shape[1] > 4 and len(in_.shape) == 3:
            last = None
            for i in range(0, out.shape[1], 4):
                o_, i_ = out[:, i:i + 4, :], in_[:, i:i + 4, :]
                last = self.P.dma(eng, (lambda o_, i_: (lambda e: e.dma_start(out=o_, in_=i_)))(o_, i_), r, w, dkey=dkey)
            return last
        return self.P.dma(eng, lambda e: e.dma_start(out=out, in_=in_), r, w, dkey=dkey)

    def build(self):
        nc = self.nc
        L = len(self.layers)
        ein = lambda n, sh, dt=F32: nc.dram_tensor(n, list(sh), dt, kind="ExternalInput").ap()
        self.xin = ein("xin", [S, D])
        self.pos = ein("pos", [128, NS], I32)
        self.pin = ein("pin", [DEPTH, S, 256])
        self.w = {}
        for n, sh in [("g_mix", [DEPTH, D]), ("w_in", [DEPTH, D, INC]), ("lambda_q1", [DEPTH, 64]), ("lambda_k1", [DEPTH, 64]),
                      ("lambda_q2", [DEPTH, 64]), ("lambda_k2", [DEPTH, 64]), ("g_subln", [DEPTH, 128]), ("g_q_latent", [DEPTH, 512]),
                      ("w_q_up", [DEPTH, 512, 1536]), ("g_kv_latent", [DEPTH, 256]), ("w_kv_up", [DEPTH, 256, 2048]),
                      ("w_branch_diff", [DEPTH, 1024, D]), ("w_branch_mla", [DEPTH, 1024, D]), ("w_out", [DEPTH, D, D]),
                      ("g_ffn", [DEPTH, D]), ("w_gate_up", [DEPTH, D, 2 * DFF]), ("w_down", [DEPTH, DFF, D]),
                      ("w_ple_in", [DEPTH, 256, D]), ("g_ple", [DEPTH, D]), ("w_ple_gate", [DEPTH, D, D]), ("g_final", [D])]:
            self.w[n] = ein(n, sh)
        self.out = nc.dram_tensor("out", [T, D], F32, kind="ExternalOutput").ap()
        self.xres = self.dram("xres", [S, D], F32)
        self.QT_d = self.dram("QT_d", [8, 128, T], BF16)
        self.KT_d = self.dram("KT_d", [8, 128, S], BF16)
        self.V_d = self.dram("V_d", [S, 1024], BF16)
        self.QnT_d = self.dram("QnT_d", [8, 128, T], BF16)
        self.QrT_d = self.dram("QrT_d", [4, 128, T], BF16)
        self.KnT_d = self.dram("KnT_d", [8, 128, S], BF16)
        self.Vm_d = self.dram("Vm_d", [S, 1024], BF16)
        self.GT_d = self.dram("GT_d", [2, KC, 128, T], F32)
        self.actT_d = self.dram("actT_d", [FC, 128, T], BF16)
        self.dbg_o = {}
        for n, sh, dt in [("dbg_hT", [128, KC, T], BF16), ("dbg_cqT", [128, 4, T], BF16), ("dbg_ckvT", [128, 2, S], BF16),
                          ("dbg_KrT2", [128, S], BF16), ("dbg_odT", [128, 8, T], BF16), ("dbg_omT", [128, 8, T], BF16),
                          ("dbg_mT", [128, KC, T], BF16), ("dbg_tab", [128, NS, 64], F32)]:
            if n in self.dbg:
                self.dbg_o[n] = nc.dram_tensor(n, sh, dt, kind="ExternalOutput").ap()

        with ExitStack() as es:
            self.arena = es.enter_context(nc.sbuf_tensor("arena", [128, ARENA_BYTES], U8))
            self.psum = [es.enter_context(nc.psum_tensor(f"ps{i}", [128, 512], F32))[:] for i in range(8)]
            pb = self.Bump(self, 0, OFF_A)
            self.ident = pb([128, 128], BF16)
            self.ones_bf = pb([128, 128], BF16)
            self.ones_f = pb([128, 128], F32)
            self.CCp = pb([128, NS, 16], F32)
            self.SSp = pb([128, NS, 16], F32)
            self.CCm = pb([128, NS, 64], F32)
            self.SSm = pb([128, NS, 64], F32)
            self.gq_b = pb([128, 512], F32)
            self.gkv_b = pb([128, 256], F32)
            self.gs_sub = pb([128, 1], F32)
            self.lamt = pb([128, 8], F32)
            self.persist_end = pb.off

            self.phase_consts()
            for (layer, src_name, own0, oth0) in self.layers:
                self.own0, self.oth0 = own0, oth0
                self.src = self.xin if src_name == "xin" else self.xres
                self.layer(layer)
            if self.final_norm:
                self.phase_final()
            else:
                self.phase_copy_out()
            self.P.emit(nc, es, final_wait_ops=self.out_ops)
        return nc

    def phase_consts(self):
        P = self.P
        b = self.Bump(self, OFF_C)
        identf = b([128, 128], F32)
        self.op("pool", "memset", [], ["identf"], ap=identf, constant=0.0)
        P.add("pool", lambda e: e.affine_select(out=identf, in_=identf, compare_op=ALU.not_equal, fill=1.0, base=0,
                                                 pattern=[[-1, 128]], channel_multiplier=1), ["identf"], ["identf"])
        self.op("pool", "tensor_copy", ["identf"], ["ident"], out=self.ident, in_=identf)
        self.op("pool", "memset", [], ["ones_bf"], ap=self.ones_bf, constant=1.0)
        self.op("pool", "memset", [], ["ones_f"], ap=self.ones_f, constant=1.0 / 128.0)
        post = b([128, NS], I32)
        posf = b([128, NS], F32)
        self.ld(post, self.pos, [], ["post"], dkey="c_pos")
        self.op("dve", "tensor_copy", ["post"], ["posf"], out=posf, in_=post)
        for nm, nf, CC, SS in (("p", 8, self.CCp, self.SSp), ("m", 32, self.CCm, self.SSm)):
            dim = 2 * nf
            invf = b([128, nf], F32)
            for j in range(nf):
                self.op("dve", "memset", [], ["invf" + nm], ap=invf[:, j:j + 1], constant=float(np.float32(ROPE_THETA) ** np.float32(-(2 * j) / dim)))
            sh = [128, NS, nf]
            ang = b(sh, F32); uu = b(sh, F32); ki = b(sh, I32); kf = b(sh, F32); r1 = b(sh, F32); r2 = b(sh, F32)
            mm_ = b(sh, F32); rs = b(sh, F32); rc = b(sh, F32); sn = b(sh, F32); cs = b(sh, F32)
            k = lambda s_: s_ + nm
            self.op("dve", "tensor_tensor", ["posf", k("invf")], [k("ang")], out=ang, in0=posf.unsqueeze(2).broadcast_to(sh),
                    in1=invf.unsqueeze(1).broadcast_to(sh), op=ALU.mult)
            self.op("dve", "tensor_scalar", [k("ang")], [k("uu")], out=uu, in0=ang, scalar1=float(1 / (2 * np.pi)), scalar2=None, op0=ALU.mult)
            self.op("dve", "tensor_copy", [k("uu")], [k("ki")], out=ki, in_=uu)
            self.op("dve", "tensor_copy", [k("ki")], [k("kf")], out=kf, in_=ki)
            C1 = 6.28125
            C2 = float(2 * np.pi - 6.28125)
            self.op("dve", "scalar_tensor_tensor", [k("kf"), k("ang")], [k("r1")], out=r1, in0=kf, scalar=-C1, in1=ang, op0=ALU.mult, op1=ALU.add)
            self.op("dve", "scalar_tensor_tensor", [k("kf"), k("r1")], [k("r2")], out=r2, in0=kf, scalar=-C2, in1=r1, op0=ALU.mult, op1=ALU.add)

            def wrap(dst, src, shift, kd, ks):
                self.op("dve", "tensor_scalar", [ks], [k("mm")], out=mm_, in0=src, scalar1=float(PI - shift), scalar2=float(-2 * PI), op0=ALU.is_ge, op1=ALU.mult)
                self.op("dve", "scalar_tensor_tensor", [ks, k("mm")], [kd], out=dst, in0=src, scalar=float(shift), in1=mm_, op0=ALU.add, op1=ALU.add)
                self.op("dve", "tensor_scalar", [kd], [kd], out=dst, in0=dst, scalar1=-PI, scalar2=PI, op0=ALU.max, op1=ALU.min)

            wrap(rs, r2, 0.0, k("rs"), k("r2"))
            wrap(rc, rs, PI / 2, k("rc"), k("rs"))
            self.act(sn, rs, AF.Sin, [k("rs")], [k("sn")])
            self.act(cs, rc, AF.Sin, [k("rc")], [k("cs")])
            tk = "tab" + nm
            self.op("dve", "tensor_copy", [k("cs")], [tk + "c0"], out=CC[:, :, 0:nf], in_=cs)
            self.op("dve", "tensor_copy", [k("cs")], [tk + "c1"], out=CC[:, :, nf:dim], in_=cs)
            self.op("dve", "tensor_scalar", [k("sn")], [tk + "s0"], out=SS[:, :, 0:nf], in0=sn, scalar1=-1.0, scalar2=None, op0=ALU.mult)
            self.op("dve", "tensor_copy", [k("sn")], [tk + "s1"], out=SS[:, :, nf:dim], in_=sn)
        if "dbg_tab" in self.dbg_o:
            self.out_ops.append(self.ld(self.dbg_o["dbg_tab"], self.CCm, ["tabmc0", "tabmc1"], [], dkey="dbg"))
        P.barrier()

    def rope(self, b_tmp, ps3, out3, CC, SS, rd, n, rk, wk, tag):
        t1, t2 = b_tmp
        hf = rd // 2
        shp = [128, n, rd]
        bc = lambda a: a.unsqueeze(1).broadcast_to([128, n, a.shape[-1]])
        self.op("dve", "tensor_tensor", rk, [tag + "t1"], out=t1, in0=ps3[:, :, 0:rd], in1=bc(CC), op=ALU.mult)
        self.op("dve", "tensor_tensor", rk, [tag + "t2a"], out=t2[:, :, 0:hf], in0=ps3[:, :, hf:rd], in1=bc(SS[:, 0:hf]), op=ALU.mult)
        self.op("dve", "tensor_tensor", rk, [tag + "t2b"], out=t2[:, :, hf:rd], in0=ps3[:, :, 0:hf], in1=bc(SS[:, hf:rd]), op=ALU.mult)
        return self.op("dve", "tensor_tensor", [tag + "t1", tag + "t2a", tag + "t2b"], wk, out=out3[:, :, 0:rd], in0=t1, in1=t2, op=ALU.add)

    def norm_pass(self, b, x_src, row0, ntiles, g_ap, hT, hT_key, tag):
        g_b = b([128, D], F32)
        xt = b([128, D], F32)
        hb = [b([128, D], BF16) for _ in range(2)]
        ss = [b([128, 1], F32) for _ in range(2)]
        self.ld(g_b, g_ap.partition_broadcast(128), [], [tag + "g_b"], dkey="n_g")
        for tt in range(ntiles):
            sl = tt % 2
            self.ld(xt, x_src[row0 + tt * 128: row0 + (tt + 1) * 128, :], [x_src.tensor.name], [tag + "xt"], dkey="n_xt")
            self.act(hb[sl], xt, AF.Square, [tag + "xt"], [tag + f"hb{sl}", tag + f"ss{sl}"], accum_out=ss[sl])
            self.act(ss[sl], ss[sl], AF.Sqrt, [tag + f"ss{sl}"], [tag + f"ss{sl}"], bias=1e-6, scale=1.0 / D)
            self.op("dve", "reciprocal", [tag + f"ss{sl}"], [tag + f"ss{sl}"], out=ss[sl], in_=ss[sl])
            self.op("dve", "scalar_tensor_tensor", [tag + "xt", tag + f"ss{sl}", tag + "g_b"], [tag + f"hb{sl}"], out=hb[sl], in0=xt,
                    scalar=ss[sl][:, 0:1], in1=g_b, op0=ALU.mult, op1=ALU.mult)
            for half in range(2):
                bank = 4 + (tt * 2 + half) % 2
                pt = self.psum[bank].bitcast(BF16).rearrange("p (a b) -> p a b", b=128)
                for j in range(8):
                    kc = half * 8 + j
                    self.tr(pt[:, j, :], hb[sl][:, kc * 128:(kc + 1) * 128], [tag + f"hb{sl}"], [f"ps{bank}"])
                dst = hT[:, half * 8:(half + 1) * 8, tt * 128:(tt + 1) * 128]
                if half == 0:
                    self.act(dst, pt, AF.Copy, [f"ps{bank}"], [hT_key])
                else:
                    self.op("dve", "tensor_copy", [f"ps{bank}"], [hT_key], out=dst, in_=pt)

    def layer(self, l):
        stop = os.environ.get("KSTOP", "")
        self.phase_z(l)
        if stop == "z":
            return
        self.phase_up(l)
        if stop == "up":
            return
        self.phase_attn(l)
        if stop == "attn":
            return
        self.phase_branch(l)
        self.phase_wout(l)
        if stop == "wout":
            return
        self.phase_ffn(l)
        if stop == "ffn":
            return
        self.phase_ple(l)

    def evac(self, i, out, in_, r, w):
        if i % 2 == 0:
            return self.act(out, in_, AF.Copy, r, w)
        return self.op("dve", "tensor_copy", r, w, out=out, in_=in_)

    def phase_up(self, l):
        P = self.P
        b = self.Bump(self, OFF_C)
        wq = b([128, 4, 1536], BF16)
        wkv = b([128, 2, 2048], BF16)
        self.ld(wq, self.w["w_q_up"][l].rearrange("(kc p) n -> p kc n", p=128), [], ["wq"], dkey="ws0", eng="pool")
        self.ld(wkv, self.w["w_kv_up"][l].rearrange("(kc p) n -> p kc n", p=128), [], ["wkv"], dkey="ws1", eng="pool")
        st = [b([128, 512], BF16) for _ in range(2)]
        rbm = [b([128, 512], BF16) for _ in range(2)]
        t1 = b([128, 8, 64], F32); t2 = b([128, 8, 64], F32)
        n = {"ps": 0, "st": 0, "rb": 0, "tp": 0}

        def nxt(k, m):
            v = n[k] % m; n[k] += 1
            return v
        wq3 = wq.rearrange("p k (h c) -> p k h c", c=192)
        wkv3 = wkv.rearrange("p k (h c) -> p k h c", c=256)
        for h in range(8):
            for tb in range(T // 512):
                bank = nxt("ps", 4); ps = self.psum[bank]
                for kc in range(4):
                    self.mm(ps, wq3[:, kc, h, 0:128], self.cqT[:, kc, tb * 512:(tb + 1) * 512], kc == 0, kc == 3, ["wq", "cqT"], [f"ps{bank}"])
                sl = nxt("st", 2)
                self.evac(sl, st[sl], ps, [f"ps{bank}"], [f"st{sl}"])
                self.ld(self.QnT_d[h, :, tb * 512:(tb + 1) * 512], st[sl], [f"st{sl}"], ["QnT_d"], dkey=f"st{sl}")
        for tt in range(NT):
            bank = nxt("ps", 4); ps = self.psum[bank]
            ps3 = ps.rearrange("p (h r) -> p h r", r=64)
            for kc in range(4):
                self.mm(ps3, self.cqT[:, kc, tt * 128:(tt + 1) * 128], wq3[:, kc, :, 128:192], kc == 0, kc == 3, ["wq", "cqT"], [f"ps{bank}"])
            r = nxt("rb", 2)
            rb3 = rbm[r].rearrange("p (h r) -> p h r", r=64)
            self.rope((t1, t2), ps3, rb3, self.CCm[:, self.own0 // 128 + tt, :], self.SSm[:, self.own0 // 128 + tt, :], 64, 8, [f"ps{bank}"], [f"rbm{r}"], "rq")
            tbank = 6 + nxt("tp", 2)
            pt = self.psum[tbank].bitcast(BF16)[:, 0:512].rearrange("p (a b) -> p a b", b=128)
            for j in range(4):
                self.tr(pt[:, j, :], rbm[r][:, j * 128:(j + 1) * 128], [f"rbm{r}"], [f"ps{tbank}"])
            sl = nxt("st", 2)
            s3 = st[sl].rearrange("p (a b) -> p a b", b=128)
            self.evac(sl, s3, pt, [f"ps{tbank}"], [f"st{sl}"])
            self.ld(self.QrT_d[:, :, tt * 128:(tt + 1) * 128].rearrange("h p t -> p h t"), s3, [f"st{sl}"], ["QrT_d"], dkey=f"st{sl}")
        for h in range(8):
            for tb in range(S // 512):
                bank = nxt("ps", 4); ps = self.psum[bank]
                for kc in range(2):
                    self.mm(ps, wkv3[:, kc, h, 0:128], self.ckvT[:, kc, tb * 512:(tb + 1) * 512], kc == 0, kc == 1, ["wkv", "ckvT"], [f"ps{bank}"])
                sl = nxt("st", 2)
                self.evac(sl, st[sl], ps, [f"ps{bank}"], [f"st{sl}"])
                self.ld(self.KnT_d[h, :, tb * 512:(tb + 1) * 512], st[sl], [f"st{sl}"], ["KnT_d"], dkey=f"st{sl}")
        for gt in range(NS):
            for half in range(2):
                bank = nxt("ps", 4); ps = self.psum[bank]
                ps3 = ps.rearrange("p (h r) -> p h r", r=128)
                for kc in range(2):
                    self.mm(ps3, self.ckvT[:, kc, gt * 128:(gt + 1) * 128], wkv3[:, kc, half * 4:(half + 1) * 4, 128:256], kc == 0, kc == 1,
                            ["wkv", "ckvT"], [f"ps{bank}"])
                sl = nxt("st", 2)
                self.evac(sl, st[sl], ps, [f"ps{bank}"], [f"st{sl}"])
                self.ld(self.Vm_d[gt * 128:(gt + 1) * 128, half * 512:(half + 1) * 512], st[sl], [f"st{sl}"], ["Vm_d"], dkey=f"st{sl}")
        P.barrier()

    def phase_attn(self, l):
        P = self.P
        lam_init = 0.8 - 0.6 * math.exp(-0.3 * l)
        self.odT, _ = self.carve(OFF_B, [128, 8, T], BF16)
        self.omT, _ = self.carve(OFF_B + 32 * 1024, [128, 8, T], BF16)
        a = self.Bump(self, OFF_A, OFF_A + 32 * 1024)
        lq = [a([128, 64], F32) for _ in range(4)]
        lt = a([128, 64], F32)
        ls = [a([128, 1], F32) for _ in range(2)]
        nlam = a([128, 1], F32)
        gsub = a([128, 1], F32)
        r1 = a([128, 512], F32); r2 = a([128, 512], F32); Av = a([128, 512], F32); Bv = a([128, 512], F32)
        sq = a([128, 512], F32); rstd = a([128, 512], F32)
        for i, nme in enumerate(("lambda_q1", "lambda_k1", "lambda_q2", "lambda_k2")):
            self.ld(lq[i], self.w[nme][l].partition_broadcast(128), [], [f"lq{i}"], dkey=f"c_l{i}")
        self.ld(gsub, self.w["g_subln"][l].rearrange("(p o) -> p o", o=1), [], ["gsub"], dkey="c_gq")
        for j in range(2):
            self.op("dve", "tensor_tensor", [f"lq{2 * j}", f"lq{2 * j + 1}"], ["lt"], out=lt, in0=lq[2 * j], in1=lq[2 * j + 1], op=ALU.mult)
            self.op("dve", "reduce_sum", ["lt"], [f"ls{j}"], out=ls[j], in_=lt, axis=mybir.AxisListType.X)
            self.act(ls[j], ls[j], AF.Exp, [f"ls{j}"], [f"ls{j}"])
        self.op("dve", "tensor_tensor", ["ls0", "ls1"], ["nlam"], out=nlam, in0=ls[1], in1=ls[0], op=ALU.subtract)
        self.op("dve", "tensor_scalar", ["nlam"], ["nlam"], out=nlam, in0=nlam, scalar1=float(-lam_init), scalar2=None, op0=ALU.add)
        self.op("dve", "tensor_scalar", ["gsub"], ["gsub"], out=gsub, in0=gsub, scalar1=float(1.0 - lam_init), scalar2=None, op0=ALU.mult)

        b = self.Bump(self, OFF_C)
        Qb = [b([128, T], BF16) for _ in range(2)]
        Q2b = [b([128, T], BF16) for _ in range(2)]
        Kb = [b([128, S], BF16) for _ in range(2)]
        Vb = [b([128, NS, 128], BF16) for _ in range(2)]
        Pt = [b([128, 512], BF16) for _ in range(4)]
        n = {"s": 0, "p": 0}

        def nxt(k, m):
            v = n[k] % m; n[k] += 1
            return v
        sc_d = 64 ** -0.5
        sc_m = 192 ** -0.5
        for h in range(8):
            hb = h % 2
            self.ld(Qb[hb], self.QT_d[h], ["QT_d"], [f"Q{hb}"], dkey=f"aq{hb}")
            self.ld(Kb[hb], self.KT_d[h], ["KT_d"], [f"K{hb}"], dkey=f"ak{hb}")
            self.ld(Vb[hb], self.V_d[:, h * 128:(h + 1) * 128].rearrange("(kt p) e -> p kt e", p=128), ["V_d"], [f"V{hb}"], dkey=f"av{hb}")
            for qb in range(T // 512):
                qs = slice(qb * 512, (qb + 1) * 512)
                O1, O2, Z1, Z2 = self.psum[4], self.psum[5], self.psum[6], self.psum[7]
                for kt in range(NS):
                    ks = slice(kt * 128, (kt + 1) * 128)
                    pts = []
                    for c in range(2):
                        bank = nxt("s", 4); ps = self.psum[bank]
                        self.mm(ps, Kb[hb][c * 64:(c + 1) * 64, ks], Qb[hb][c * 64:(c + 1) * 64, qs], True, True, [f"K{hb}", f"Q{hb}"], [f"ps{bank}"])
                        p = nxt("p", 4)
                        self.act(Pt[p], ps, AF.Exp, [f"ps{bank}"], [f"Pt{p}"], scale=sc_d)
                        pts.append(p)
                    for c, (O, Z) in enumerate(((O1, Z1), (O2, Z2))):
                        p = pts[c]
                        self.mm(O, Vb[hb][:, kt, :], Pt[p], kt == 0, kt == NS - 1, [f"V{hb}", f"Pt{p}"], [f"ps{4 + c}"])
                        self.mm(Z, self.ones_bf, Pt[p], kt == 0, kt == NS - 1, ["ones_bf", f"Pt{p}"], [f"ps{6 + c}"])
                self.op("dve", "reciprocal", ["ps6"], ["r1"], out=r1, in_=Z1)
                self.op("dve", "reciprocal", ["ps7"], ["r2"], out=r2, in_=Z2)
                self.op("dve", "tensor_tensor", ["ps4", "r1"], ["Av"], out=Av, in0=O1, in1=r1, op=ALU.mult)
                self.op("dve", "tensor_tensor", ["ps5", "r2"], ["Bv"], out=Bv, in0=O2, in1=r2, op=ALU.mult)
                self.op("dve", "scalar_tensor_tensor", ["Bv", "Av", "nlam"], ["Av2"], out=Av, in0=Bv, scalar=nlam[:, 0:1], in1=Av, op0=ALU.mult, op1=ALU.add)
                self.op("pool", "tensor_tensor", ["Av2"], ["sq"], out=sq, in0=Av, in1=Av, op=ALU.mult)
                bank = nxt("s", 4); ms = self.psum[bank]
                self.mm(ms, self.ones_f, sq, True, True, ["ones_f", "sq"], [f"ps{bank}"])
                self.act(rstd, ms, AF.Sqrt, [f"ps{bank}"], ["rstd"], bias=1e-5, scale=1.0)
                self.op("dve", "reciprocal", ["rstd"], ["rstd"], out=rstd, in_=rstd)
                self.op("dve", "scalar_tensor_tensor", ["Av2", "gsub", "rstd"], ["odT"], out=self.odT[:, h, qs], in0=Av, scalar=gsub[:, 0:1], in1=rstd,
                        op0=ALU.mult, op1=ALU.mult)
        for h in range(8):
            hb = h % 2
            p0 = 64 * (h % 2)
            self.ld(Qb[hb], self.QnT_d[h], ["QnT_d"], [f"Q{hb}"], dkey=f"aq{hb}")
            self.ld(Q2b[hb], self.QrT_d[h // 2], ["QrT_d"], [f"Q2{hb}"], dkey=f"aq2{hb}")
            self.ld(Kb[hb], self.KnT_d[h], ["KnT_d"], [f"K{hb}"], dkey=f"ak{hb}")
            self.ld(Vb[hb], self.Vm_d[:, h * 128:(h + 1) * 128].rearrange("(kt p) e -> p kt e", p=128), ["Vm_d"], [f"V{hb}"], dkey=f"av{hb}")
            for qb in range(T // 512):
                qs = slice(qb * 512, (qb + 1) * 512)
                ob = 4 + 2 * (qb % 2)
                O, Z = self.psum[ob], self.psum[ob + 1]
                for kt in range(NS):
                    ks = slice(kt * 128, (kt + 1) * 128)
                    bank = nxt("s", 4); ps = self.psum[bank]
                    self.mm(ps, Kb[hb][:, ks], Qb[hb][:, qs], True, False, [f"K{hb}", f"Q{hb}"], [f"ps{bank}"])
                    self.mm(ps, self.KrT2[p0:p0 + 64, ks], Q2b[hb][p0:p0 + 64, qs], False, True, ["KrT2", f"Q2{hb}"], [f"ps{bank}"])
                    p = nxt("p", 4)
                    self.act(Pt[p], ps, AF.Exp, [f"ps{bank}"], [f"Pt{p}"], scale=sc_m)
                    self.mm(O, Vb[hb][:, kt, :], Pt[p], kt == 0, kt == NS - 1, [f"V{hb}", f"Pt{p}"], [f"ps{ob}"])
                    self.mm(Z, self.ones_bf, Pt[p], kt == 0, kt == NS - 1, ["ones_bf", f"Pt{p}"], [f"ps{ob + 1}"])
                self.op("dve", "reciprocal", [f"ps{ob + 1}"], ["r1"], out=r1, in_=Z)
                self.op("dve", "tensor_tensor", [f"ps{ob}", "r1"], ["omT"], out=self.omT[:, h, qs], in0=O, in1=r1, op=ALU.mult)
        for nme, t_, k_ in (("dbg_odT", self.odT, "odT"), ("dbg_omT", self.omT, "omT")):
            if nme in self.dbg_o:
                self.out_ops.append(self.ld(self.dbg_o[nme], t_, [k_], [], dkey="dbg"))
        P.barrier()

    def phase_branch(self, l):
        P = self.P
        self.mT, _ = self.carve(OFF_C, [128, KC, T], BF16)
        a = self.Bump(self, OFF_A, OFF_B)
        wbd = [a([128, 8, 128], BF16) for _ in range(2)]
        wbm = [a([128, 8, 128], BF16) for _ in range(2)]
        sga = [a([128, 512], F32) for _ in range(2)]
        sgb = [a([128, 512], F32) for _ in range(2)]
        m1 = [a([128, 512], F32) for _ in range(2)]
        m2 = [a([128, 512], F32) for _ in range(2)]
        it = 0
        for cc in range(KC):
            ws = cc % 2
            cs = slice(cc * 128, (cc + 1) * 128)
            self.ld(wbd[ws], self.w["w_branch_diff"][l][:, cs].rearrange("(h p) n -> p h n", p=128), [], [f"wbd{ws}"], dkey=f"ws{ws}", eng="pool")
            self.ld(wbm[ws], self.w["w_branch_mla"][l][:, cs].rearrange("(h p) n -> p h n", p=128), [], [f"wbm{ws}"], dkey=f"wsb{ws}", eng="pool")
            for tb in range(T // 512):
                ts_ = slice(tb * 512, (tb + 1) * 512)
                g = it % 2; it += 1
                bd, bm = 2 * g, 2 * g + 1
                self.ld(sga[g], self.GT_d[0, cc, :, ts_], ["GT_d"], [f"sga{g}"], dkey=f"ga{g}")
                self.ld(sgb[g], self.GT_d[1, cc, :, ts_], ["GT_d"], [f"sgb{g}"], dkey=f"gb{g}")
                for h in range(8):
                    self.mm(self.psum[bd], wbd[ws][:, h, :], self.odT[:, h, ts_], h == 0, h == 7, [f"wbd{ws}", "odT"], [f"ps{bd}"])
                for h in range(8):
                    self.mm(self.psum[bm], wbm[ws][:, h, :], self.omT[:, h, ts_], h == 0, h == 7, [f"wbm{ws}", "omT"], [f"ps{bm}"])
                self.op("dve", "tensor_tensor", [f"ps{bd}", f"sga{g}"], [f"m1{g}"], out=m1[g], in0=self.psum[bd], in1=sga[g], op=ALU.mult)
                self.op("dve", "tensor_tensor", [f"ps{bm}", f"sgb{g}"], [f"m2{g}"], out=m2[g], in0=self.psum[bm], in1=sgb[g], op=ALU.mult)
                self.op("pool", "tensor_tensor", [f"m1{g}", f"m2{g}"], ["mT"], out=self.mT[:, cc, ts_], in0=m1[g], in1=m2[g], op=ALU.add)
        if "dbg_mT" in self.dbg_o:
            self.out_ops.append(self.ld(self.dbg_o["dbg_mT"], self.mT, ["mT"], [], dkey="dbg"))
        P.barrier()

    def gemm_resid(self, lhsT_fn, nk, w_ap, w_pat, x_src, src0, tagk, pre=None):
        a = self.Bump(self, OFF_A, OFF_C)
        ws = [a([128, nk, 512], BF16) for _ in range(2)]
        xo = [a([128, 512], F32) for _ in range(2)]
        xn = [a([128, 512], F32) for _ in range(2)]
        it = 0
        for cg in range(D // 512):
            sl = cg % 2
            cs = slice(cg * 512, (cg + 1) * 512)
            self.ld(ws[sl], w_ap[:, cs].rearrange(w_pat, p=128), [], [f"ws{sl}"], dkey=f"ws{sl}", eng="pool")
            for tt in range(NT):
                rs = slice(tt * 128, (tt + 1) * 128)
                g = it % 2; it += 1
                bank = it % 4
                self.ld(xo[g], x_src[src0 + tt * 128:src0 + (tt + 1) * 128, cs], [x_src.tensor.name], [f"xo{g}"], dkey=f"xo{g}")
                for k in range(nk):
                    lt, rk = lhsT_fn(tt, k)
                    self.mm(self.psum[bank], lt, ws[sl][:, k, :], k == 0, k == nk - 1, [f"ws{sl}"] + rk, [f"ps{bank}"])
                self.op("dve", "tensor_tensor", [f"ps{bank}", f"xo{g}"], [f"xn{g}"], out=xn[g], in0=self.psum[bank], in1=xo[g], op=ALU.add)
                self.ld(self.xres[self.own0 + tt * 128:self.own0 + (tt + 1) * 128, cs], xn[g], [f"xn{g}"], ["xres"], dkey=f"xn{g}")

    def phase_wout(self, l):
        self.gemm_resid(lambda tt, k: (self.mT[:, k, tt * 128:(tt + 1) * 128], ["mT"]), KC, self.w["w_out"][l], "(kc p) n -> p kc n", self.src, self.own0, "wo")
        self.P.barrier()

    def phase_ffn(self, l):
        P = self.P
        hT, _ = self.carve(OFF_B, [128, KC, T], BF16)
        b = self.Bump(self, OFF_C)
        self.norm_pass(b, self.xres, self.own0, NT, self.w["g_ffn"][l], hT, "hT", f"f{l}")
        wg = [b([128, KC, 512], BF16) for _ in range(1)]
        wu = [b([128, KC, 512], BF16) for _ in range(1)]
        a = self.Bump(self, OFF_A, OFF_B)
        wg.append(a([128, KC, 512], BF16)); wu.append(a([128, KC, 512], BF16))
        sg = [a([128, 512], F32) for _ in range(2)]
        av = [a([128, 512], BF16) for _ in range(2)]
        Wgu = self.w["w_gate_up"][l]
        it = 0
        for fg in range(DFF // 512):
            sl = fg % 2
            self.ld(wg[sl], Wgu[:, fg * 512:(fg + 1) * 512].rearrange("(kc p) n -> p kc n", p=128), [], [f"wg{sl}"], dkey=f"ws{sl}", eng="pool")
            self.ld(wu[sl], Wgu[:, DFF + fg * 512:DFF + (fg + 1) * 512].rearrange("(kc p) n -> p kc n", p=128), [], [f"wu{sl}"], dkey=f"wsb{sl}", eng="pool")
            for j in range(4):
                fc = fg * 4 + j
                for tb in range(T // 512):
                    ts_ = slice(tb * 512, (tb + 1) * 512)
                    g = it % 2; it += 1
                    bg, bu = 2 * g, 2 * g + 1
                    for kc in range(KC):
                        self.mm(self.psum[bg], wg[sl][:, kc, j * 128:(j + 1) * 128], hT[:, kc, ts_], kc == 0, kc == KC - 1, [f"wg{sl}", "hT"], [f"ps{bg}"])
                    for kc in range(KC):
                        self.mm(self.psum[bu], wu[sl][:, kc, j * 128:(j + 1) * 128], hT[:, kc, ts_], kc == 0, kc == KC - 1, [f"wu{sl}", "hT"], [f"ps{bu}"])
                    self.act(sg[g], self.psum[bg], AF.Silu, [f"ps{bg}"], [f"sg{g}"])
                    self.op("dve", "tensor_tensor", [f"sg{g}", f"ps{bu}"], [f"av{g}"], out=av[g], in0=sg[g], in1=self.psum[bu], op=ALU.mult)
                    self.ld(self.actT_d[fc, :, ts_], av[g], [f"av{g}"], ["actT_d"], dkey=f"av{g}")
        P.barrier()
        a = self.Bump(self, OFF_A)
        wd = [a([128, FC, 512], BF16) for _ in range(2)]
        at = [a([128, FC, 128], BF16) for _ in range(2)]
        xo = [a([128, 512], F32) for _ in range(2)]
        xn = [a([128, 512], F32) for _ in range(2)]
        it = 0
        for cg in range(D // 512):
            sl = cg % 2
            cs = slice(cg * 512, (cg + 1) * 512)
            self.ld(wd[sl], self.w["w_down"][l][:, cs].rearrange("(fc p) n -> p fc n", p=128), [], [f"wd{sl}"], dkey=f"ws{sl}", eng="pool")
            for tt in range(NT):
                rs = slice(tt * 128, (tt + 1) * 128)
                g = it % 2; it += 1
                bank = it % 4
                xs_ = slice(self.own0 + tt * 128, self.own0 + (tt + 1) * 128)
                self.ld(at[g], self.actT_d[:, :, rs].rearrange("f p t -> p f t"), ["actT_d"], [f"at{g}"], dkey=f"at{g}")
                self.ld(xo[g], self.xres[xs_, cs], ["xres"], [f"xo{g}"], dkey=f"xo{g}")
                for fc in range(FC):
                    self.mm(self.psum[bank], at[g][:, fc, :], wd[sl][:, fc, :], fc == 0, fc == FC - 1, [f"wd{sl}", f"at{g}"], [f"ps{bank}"])
                self.op("dve", "tensor_tensor", [f"ps{bank}", f"xo{g}"], [f"xn{g}"], out=xn[g], in0=self.psum[bank], in1=xo[g], op=ALU.add)
                self.ld(self.xres[xs_, cs], xn[g], [f"xn{g}"], ["xres"], dkey=f"xn{g}")
        P.barrier()

    def phase_ple(self, l):
        P = self.P
        hT, _ = self.carve(OFF_B, [128, KC, T], BF16)
        b = self.Bump(self, OFF_C)
        self.norm_pass(b, self.xres, self.own0, NT, self.w["g_ple"][l], hT, "hT", f"p{l}")
        a = self.Bump(self, OFF_A, OFF_B)
        pT = a([128, 2, T], BF16)
        pf = [a([128, 256], F32) for _ in range(2)]
        pb = [a([128, 256], BF16) for _ in range(2)]
        for tt in range(NT):
            g = tt % 2
            self.ld(pf[g], self.pin[l, self.own0 + tt * 128:self.own0 + (tt + 1) * 128, :], [], [f"pf{g}"], dkey=f"xo{g}")
            self.op("pool", "tensor_copy", [f"pf{g}"], [f"pb{g}"], out=pb[g], in_=pf[g])
            tbank = 6 + tt % 2
            pt = self.psum[tbank].bitcast(BF16)[:, 0:256].rearrange("p (a b) -> p a b", b=128)
            for j in range(2):
                self.tr(pt[:, j, :], pb[g][:, j * 128:(j + 1) * 128], [f"pb{g}"], [f"ps{tbank}"])
            self.evac(tt, pT[:, :, tt * 128:(tt + 1) * 128], pt, [f"ps{tbank}"], ["pT"])
        ws = [b([128, KC, 512], BF16) for _ in range(2)]
        wi = [b([128, 2, 512], BF16) for _ in range(2)]
        sg = [a([128, 512], F32) for _ in range(2)]
        xo = [a([128, 512], F32) for _ in range(2)]
        xn = [a([128, 512], F32) for _ in range(2)]
        it = 0
        for cg in range(D // 512):
            sl = cg % 2
            cs = slice(cg * 512, (cg + 1) * 512)
            self.ld(ws[sl], self.w["w_ple_gate"][l][:, cs].rearrange("(kc p) n -> p kc n", p=128), [], [f"ws{sl}"], dkey=f"ws{sl}", eng="pool")
            self.ld(wi[sl], self.w["w_ple_in"][l][:, cs].rearrange("(kc p) n -> p kc n", p=128), [], [f"wi{sl}"], dkey=f"wsb{sl}", eng="pool")
            for tt in range(NT):
                rs = slice(tt * 128, (tt + 1) * 128)
                g = it % 2; it += 1
                bg, be = 2 * g, 2 * g + 1
                xs_ = slice(self.own0 + tt * 128, self.own0 + (tt + 1) * 128)
                self.ld(xo[g], self.xres[xs_, cs], ["xres"], [f"xo{g}"], dkey=f"xo{g}")
                for kc in range(KC):
                    self.mm(self.psum[bg], hT[:, kc, rs], ws[sl][:, kc, :], kc == 0, kc == KC - 1, [f"ws{sl}", "hT"], [f"ps{bg}"])
                for kc in range(2):
                    self.mm(self.psum[be], pT[:, kc, rs], wi[sl][:, kc, :], kc == 0, kc == 1, [f"wi{sl}", "pT"], [f"ps{be}"])
                self.act(sg[g], self.psum[bg], AF.Sigmoid, [f"ps{bg}"], [f"sg{g}"])
                self.op("dve", "tensor_tensor", [f"sg{g}", f"ps{be}"], [f"sg{g}"], out=sg[g], in0=sg[g], in1=self.psum[be], op=ALU.mult)
                self.op("pool", "tensor_tensor", [f"sg{g}", f"xo{g}"], [f"xn{g}"], out=xn[g], in0=sg[g], in1=xo[g], op=ALU.add)
                self.ld(self.xres[xs_, cs], xn[g], [f"xn{g}"], ["xres"], dkey=f"xn{g}")
        P.barrier()

    def phase_z(self, l):
        P = self.P
        W = self.w["w_in"]
        a = self.Bump(self, OFF_A, OFF_B)
        self.cqT = a([128, 4, T], BF16)
        self.ckvT = a([128, 2, S], BF16)
        self.KrT2 = a([128, S], BF16)
        self.ld(self.gq_b, self.w["g_q_latent"][l].partition_broadcast(128), [], ["gq_b"], dkey="c_gq")
        self.ld(self.gkv_b, self.w["g_kv_latent"][l].partition_broadcast(128), [], ["gkv_b"], dkey="c_gkv")
        for ps_ in range(2):
            tok0 = ps_ * T
            hT, _ = self.carve(OFF_B, [128, KC, T], BF16)
            b = self.Bump(self, OFF_C)
            tag = f"z{l}{ps_}"
            self.norm_pass(b, self.src, (self.own0 if ps_ == 0 else self.oth0), NT, self.w["g_mix"][l], hT, "hT", tag)
            if "dbg_hT" in self.dbg_o and ps_ == 0 and l == 0 and self.own0 == 0:
                self.out_ops.append(self.ld(self.dbg_o["dbg_hT"], hT, ["hT"], [], dkey="dbg"))
            ws = [b([128, KC, 512], BF16) for _ in range(2)]
            rb = [b([128, 512], BF16) for _ in range(2)]
            t1 = b([128, 8, 16], F32); t2 = b([128, 8, 16], F32)
            t1m = b([128, 1, 64], F32); t2m = b([128, 1, 64], F32)
            qts = [b([128, 512], BF16) for _ in range(2)]
            gst = [b([128, 512], F32) for _ in range(2)]
            cqf = b([128, 512], F32)
            cqs = b([128, 1], F32)
            cqj = b([128, 512], BF16)
            krb = b([128, 128], BF16)
            groups = []
            if ps_ == 0:
                groups += [("dq", C_DQ, 512), ("dq", C_DQ + 512, 512)]
            groups += [("dk", C_DK, 512), ("dk", C_DK + 512, 512), ("dv", C_DV, 512), ("dv", C_DV + 512, 512)]
            if ps_ == 0:
                groups += [("cq", C_CQ, 512)]
            groups += [("ckv", C_CKV, 320)]
            if ps_ == 0:
                groups += [("g", C_GA + i * 512, 512) for i in range(8)]
            cnt = {"ps": 0, "rb": 0, "qts": 0, "gst": 0}
            for gi, (kind, c0, ncol) in enumerate(groups):
                sl = gi % 2
                wsl = ws[sl][:, :, 0:ncol]
                self.ld(wsl, W[l, :, c0:c0 + ncol].rearrange("(kc p) n -> p kc n", p=128), [], [f"ws{sl}"], dkey=f"ws{sl}", eng="pool")
                if kind == "g":
                    gsel, chunk0 = divmod((c0 - C_GA) // 128, KC)
                    for cc in range(4):
                        for tb in range(T // 512):
                            bank = cnt["ps"] % 4; cnt["ps"] += 1
                            ps = self.psum[bank]
                            for kc in range(KC):
                                self.mm(ps, wsl[:, kc, cc * 128:(cc + 1) * 128], hT[:, kc, tb * 512:(tb + 1) * 512], kc == 0, kc == KC - 1,
                                        [f"ws{sl}", "hT"], [f"ps{bank}"])
                            gs = cnt["gst"] % 2; cnt["gst"] += 1
                            self.act(gst[gs], ps, AF.Sigmoid, [f"ps{bank}"], [f"gst{gs}"])
                            self.ld(self.GT_d[gsel, chunk0 + cc, :, tb * 512:(tb + 1) * 512], gst[gs], [f"gst{gs}"], ["GT_d"], dkey=f"gst{gs}")
                    continue
                for tt in range(NT):
                    gt = ps_ * NT + tt
                    tab = (self.own0 if ps_ == 0 else self.oth0) // 128 + tt
                    bank = cnt["ps"] % 4; cnt["ps"] += 1
                    ps = self.psum[bank][:, 0:ncol]
                    for kc in range(KC):
                        self.mm(ps, hT[:, kc, tt * 128:(tt + 1) * 128], wsl[:, kc, :], kc == 0, kc == KC - 1, [f"ws{sl}", "hT"], [f"ps{bank}"])
                    pk = [f"ps{bank}"]
                    if kind in ("dq", "dk"):
                        r = cnt["rb"] % 2; cnt["rb"] += 1
                        self.act(rb[r], ps, AF.Copy, pk, [f"rb{r}"])
                        ps3 = ps.rearrange("p (s d) -> p s d", d=64)
                        rb3 = rb[r].rearrange("p (s d) -> p s d", d=64)
                        self.rope((t1, t2), ps3, rb3, self.CCp[:, tab, :], self.SSp[:, tab, :], 16, 8, pk + ([f"rb{r}"] if os.environ.get("KSER") else []), [f"rb{r}"], "rp")
                        tbank = TB0 + cnt["qts"] % 2
                        q = cnt["qts"] % 2; cnt["qts"] += 1
                        pt = self.psum[tbank].bitcast(BF16)[:, 0:512].rearrange("p (a b) -> p a b", b=128)
                        for j in range(4):
                            self.tr(pt[:, j, :], rb[r][:, j * 128:(j + 1) * 128], [f"rb{r}"], [f"ps{tbank}"])
                        q3 = qts[q].rearrange("p (a b) -> p a b", b=128)
                        self.op("dve", "tensor_copy", [f"ps{tbank}"], [f"qts{q}"], out=q3, in_=pt)
                        h0 = ((c0 - (C_DQ if kind == "dq" else C_DK)) // 128)
                        if kind == "dq":
                            dst = self.QT_d[h0:h0 + 4, :, tt * 128:(tt + 1) * 128].rearrange("h p t -> p h t")
                            self.ld(dst, q3, [f"qts{q}"], ["QT_d"], dkey=f"qts{q}")
                        else:
                            dst = self.KT_d[h0:h0 + 4, :, gt * 128:(gt + 1) * 128].rearrange("h p t -> p h t")
                            self.ld(dst, q3, [f"qts{q}"], ["KT_d"], dkey=f"qts{q}")
                    elif kind == "dv":
                        r = cnt["rb"] % 2; cnt["rb"] += 1
                        self.act(rb[r], ps, AF.Copy, pk, [f"rb{r}"])
                        self.ld(self.V_d[gt * 128:(gt + 1) * 128, c0 - C_DV:c0 - C_DV + 512], rb[r], [f"rb{r}"], ["V_d"], dkey=f"rb{r}")
                    elif kind == "cq":
                        self.act(cqj, ps, AF.Square, pk, ["cqj", "cqs"], accum_out=cqs)
                        self.act(cqs, cqs, AF.Sqrt, ["cqs"], ["cqs"], bias=1e-6, scale=1.0 / 512)
                        self.op("dve", "reciprocal", ["cqs"], ["cqs"], out=cqs, in_=cqs)
                        r = cnt["rb"] % 2; cnt["rb"] += 1
                        self.op("dve", "scalar_tensor_tensor", pk + ["cqs", "gq_b"], [f"rb{r}"], out=rb[r], in0=ps, scalar=cqs[:, 0:1], in1=self.gq_b,
                                op0=ALU.mult, op1=ALU.mult)
                        tbank = TB0 + cnt["qts"] % 2; cnt["qts"] += 1
                        pt = self.psum[tbank].bitcast(BF16)[:, 0:512].rearrange("p (a b) -> p a b", b=128)
                        for j in range(4):
                            self.tr(pt[:, j, :], rb[r][:, j * 128:(j + 1) * 128], [f"rb{r}"], [f"ps{tbank}"])
                        self.op("dve", "tensor_copy", [f"ps{tbank}"], ["cqT"], out=self.cqT[:, :, tt * 128:(tt + 1) * 128], in_=pt)
                    elif kind == "ckv":
                        self.act(cqj[:, 0:256], ps[:, 0:256], AF.Square, pk, ["cqj", "cqs"], accum_out=cqs)
                        self.act(cqs, cqs, AF.Sqrt, ["cqs"], ["cqs"], bias=1e-6, scale=1.0 / 256)
                        self.op("dve", "reciprocal", ["cqs"], ["cqs"], out=cqs, in_=cqs)
                        r = cnt["rb"] % 2; cnt["rb"] += 1
                        self.op("dve", "scalar_tensor_tensor", pk + ["cqs", "gkv_b"], [f"rb{r}"], out=rb[r][:, 0:256], in0=ps[:, 0:256], scalar=cqs[:, 0:1],
                                in1=self.gkv_b, op0=ALU.mult, op1=ALU.mult)
                        ps3 = ps[:, 256:320].rearrange("p (s d) -> p s d", d=64)
                        k3 = krb[:, 0:64].rearrange("p (s d) -> p s d", d=64)
                        self.rope((t1m, t2m), ps3, k3, self.CCm[:, tab, :], self.SSm[:, tab, :], 64, 1, pk, ["krb"], "rm")
                        self.op("dve", "tensor_copy", ["krb"], ["krb"], out=krb[:, 64:128], in_=krb[:, 0:64])
                        tbank = TB0 + cnt["qts"] % 2; cnt["qts"] += 1
                        pt = self.psum[tbank].bitcast(BF16)[:, 0:384].rearrange("p (a b) -> p a b", b=128)
                        for j in range(2):
                            self.tr(pt[:, j, :], rb[r][:, j * 128:(j + 1) * 128], [f"rb{r}"], [f"ps{tbank}"])
                        self.tr(pt[:, 2, :], krb, ["krb"], [f"ps{tbank}"])
                        self.op("dve", "tensor_copy", [f"ps{tbank}"], ["ckvT"], out=self.ckvT[:, :, gt * 128:(gt + 1) * 128], in_=pt[:, 0:2, :])
                        self.act(self.KrT2[:, gt * 128:(gt + 1) * 128], pt[:, 2, :], AF.Copy, [f"ps{tbank}"], ["KrT2"])
            P.barrier()
        for n, t_, k_ in (("dbg_cqT", self.cqT, "cqT"), ("dbg_ckvT", self.ckvT, "ckvT"), ("dbg_KrT2", self.KrT2, "KrT2")):
            if n in self.dbg_o:
                self.out_ops.append(self.ld(self.dbg_o[n], t_, [k_], [], dkey="dbg"))

    def phase_final(self):
        b = self.Bump(self, OFF_A)
        g_b = b([128, D], F32)
        xt = [b([128, D], F32) for _ in range(2)]
        yt = [b([128, D], F32) for _ in range(2)]
        jk = b([128, D], BF16)
        ss = [b([128, 1], F32) for _ in range(2)]
        self.ld(g_b, self.w["g_final"].partition_broadcast(128), [], ["fg_b"], dkey="n_g")
        for tt in range(NT):
            g = tt % 2
            rs = slice(tt * 128, (tt + 1) * 128)
            self.ld(xt[g], self.xres[rs, :], ["xres"], [f"fxt{g}"], dkey=f"xo{g}")
            self.act(jk, xt[g], AF.Square, [f"fxt{g}"], ["fjk", f"fss{g}"], accum_out=ss[g])
            self.act(ss[g], ss[g], AF.Sqrt, [f"fss{g}"], [f"fss{g}"], bias=1e-6, scale=1.0 / D)
            self.op("dve", "reciprocal", [f"fss{g}"], [f"fss{g}"], out=ss[g], in_=ss[g])
            self.op("dve", "scalar_tensor_tensor", [f"fxt{g}", f"fss{g}", "fg_b"], [f"fyt{g}"], out=yt[g], in0=xt[g], scalar=ss[g][:, 0:1], in1=g_b,
                    op0=ALU.mult, op1=ALU.mult)
            self.out_ops.append(self.ld(self.out[rs, :], yt[g], [f"fyt{g}"], [], dkey=f"xn{g}"))

    def phase_copy_out(self):
        for i in range(4):
            rs = slice(i * 512, (i + 1) * 512)
            self.out_ops.append(self.ld(self.out[rs, :], self.xres[rs, :], ["xres"], [], dkey=f"co{i}"))


_WNAMES = ["g_mix", "w_in", "lambda_q1", "lambda_k1", "lambda_q2", "lambda_k2", "g_subln", "g_q_latent", "w_q_up", "g_kv_latent",
           "w_kv_up", "w_branch_diff", "w_branch_mla", "w_out", "g_ffn", "w_gate_up", "w_down", "w_ple_in", "g_ple", "w_ple_gate", "g_final"]


def core_inputs(c, x_full, p, positions, weights):
    b, hf = divmod(c, 2)
    own = slice(hf * T, (hf + 1) * T)
    oth = slice((1 - hf) * T, (2 - hf) * T)
    xin = np.ascontiguousarray(np.concatenate([x_full[b, own], x_full[b, oth]], axis=0))
    pos = np.concatenate([positions[b, own], positions[b, oth]], axis=0).astype(np.int32)
    pos = np.ascontiguousarray(pos.reshape(NS, 128).T)
    pin = np.ascontiguousarray(np.concatenate([p[:, b, own, :], p[:, b, oth, :]], axis=1))
    d = {"xin": xin, "pos": pos, "pin": pin}
    d.update(weights)
    return d


def kernel(**inputs):
    x = np.asarray(inputs["x"], dtype=np.float32)
    p = np.asarray(inputs["p"], dtype=np.float32)
    positions = np.asarray(inputs["positions"])
    weights = {n: np.ascontiguousarray(np.asarray(inputs[n], dtype=np.float32)) for n in _WNAMES}
    nc = Builder([(0, "xin", 0, T), (0, "xin", T, 0), (1, "xres", 0, T)], final_norm=True).build()
    in_maps = [core_inputs(c, x, p, positions, weights) for c in range(8)]
    res = run_bass_kernel_spmd(nc, in_maps, core_ids=list(range(8)))
    out = np.empty_like(x)
    for c in range(8):
        b, hf = divmod(c, 2)
        out[b, hf * T:(hf + 1) * T] = res.results[c]["out"]
    return out
```

```python
import math
import os
from contextlib import ExitStack

import numpy as np
import concourse.bass as bass
import concourse.mybir as mybir
from concourse.bass_utils import run_bass_kernel_spmd

F32 = mybir.dt.float32
BF16 = mybir.dt.bfloat16
I32 = mybir.dt.int32
U8 = mybir.dt.uint8
AF = mybir.ActivationFunctionType
ALU = mybir.AluOpType
PI = float(np.pi)

D = 2048
T = 2048
S = 4096
NT = T // 128
NS = S // 128
KC = D // 128
INC = 8000
DFF = 5632
FC = DFF // 128
DEPTH = 2
C_DQ, C_DK, C_DV, C_CQ, C_CKV, C_KR, C_GA, C_GB = 0, 1024, 2048, 3072, 3584, 3840, 3904, 5952
ROPE_THETA = 500000.0
TB0 = int(os.environ.get("KTB0", "6"))


class Op:
    __slots__ = ("eng", "fn", "deps", "is_dma", "dkey", "idx", "sem", "val", "signal", "name")

    def __init__(self, eng, fn, is_dma=False, dkey=None, name=""):
        self.eng = eng
        self.fn = fn
        self.deps = []
        self.is_dma = is_dma
        self.dkey = dkey
        self.sem = None
        self.val = None
        self.signal = False
        self.name = name


class Plan:
    ENGS = ("pe", "act", "dve", "pool", "sp")

    def __init__(self):
        self.ops = []
        self.last_w = {}
        self.readers = {}
        self.last_eng = {}
        self.last_dma = {}
        self.bar = {}

    def add(self, eng, fn, reads=(), writes=(), is_dma=False, dkey=None, name="", extra_deps=(), exclude=()):
        op = Op(eng, fn, is_dma, dkey, name)
        op.idx = len(self.ops)
        deps = set()
        extra = list(extra_deps)
        if eng in self.bar:
            extra += self.bar.pop(eng)
        xr = []
        for b in reads:
            w = self.last_w.get(b)
            if w is not None:
                deps.add(w)
            if isinstance(b, str) and b.startswith("ps"):
                for r_ in self.readers.get(b, ()):
                    if r_.eng != eng:
                        xr.append(r_)
        extra += xr
        for b in writes:
            w = self.last_w.get(b)
            if w is not None:
                deps.add(w)
            for r in self.readers.get(b, ()):
                deps.add(r)
        extra = [d for d in extra if d is not None]
        for d in extra:
            deps.add(d)
        pruned = []
        for d in deps:
            if d in exclude:
                continue
            if (not d.is_dma) and (not op.is_dma) and d.eng == op.eng and d.eng == "pe":
                continue
            pruned.append(d)
        op.deps = sorted(pruned, key=lambda o: o.idx)
        for d in op.deps:
            d.signal = True
        for b in reads:
            self.readers.setdefault(b, []).append(op)
        for b in writes:
            self.last_w[b] = op
            self.readers[b] = []
        self.ops.append(op)
        if is_dma:
            self.last_dma[dkey] = op
        else:
            self.last_eng[eng] = op
        return op

    def dma(self, eng, fn, reads=(), writes=(), dkey=None, name="", extra_deps=(), exclude=()):
        assert dkey is not None
        return self.add(eng, fn, reads, writes, is_dma=True, dkey=dkey, name=name, extra_deps=extra_deps, exclude=exclude)

    def barrier(self):
        if os.environ.get("KMARK"):
            print("barrier at op", len(self.ops))
        deps = list(self.last_eng.values()) + list(self.last_dma.values())
        self.bar = {e: list(deps) for e in self.ENGS}

    def emit(self, nc, es, final_wait_ops=()):
        sems = {}

        def get_sem(key):
            if key not in sems:
                nm = "s_" + "_".join(str(k) for k in (key if isinstance(key, tuple) else (key,)))
                sems[key] = es.enter_context(nc.semaphore(nm))
            return sems[key]

        maxops = int(os.environ.get("KMAXOPS", "0"))
        if maxops:
            self.ops = self.ops[:maxops]
            final_wait_ops = [o for o in final_wait_ops if o.idx < maxops]
            for o in self.ops:
                o.signal = False
            for o in self.ops:
                for d_ in o.deps:
                    d_.signal = True
        counters = {}
        for op in self.ops:
            if op.is_dma:
                key = ("d", op.dkey)
                op.signal = True
                counters[key] = counters.get(key, 0) + 16
            else:
                if not op.signal:
                    continue
                key = ("e", op.eng)
                counters[key] = counters.get(key, 0) + 1
            op.sem = get_sem(key)
            op.val = counters[key]
        self.n_sems = len(sems)
        per_eng = {e: [] for e in self.ENGS}
        for op in self.ops:
            per_eng[op.eng].append(op)
        final_wait_ops = list(final_wait_ops)
        block = es.enter_context(nc.Block())

        def run(eng_name, eng):
            known = {}

            def waits(dl):
                need = {}
                for d in dl:
                    k = id(d.sem)
                    if known.get(k, 0) >= d.val:
                        continue
                    if k not in need or need[k][1] < d.val:
                        need[k] = (d.sem, d.val)
                for k, (s, v) in need.items():
                    eng.wait_ge(s, v)
                    known[k] = v

            for op in per_eng[eng_name]:
                waits(op.deps)
                ins = op.fn(eng)
                if op.signal:
                    ins.then_inc(op.sem, 16 if op.is_dma else 1)
            if eng_name == "sp":
                waits(final_wait_ops)

        @block.tensor
        def _(e):
            run("pe", e)

        @block.scalar
        def _(e):
            run("act", e)

        @block.vector
        def _(e):
            run("dve", e)

        @block.gpsimd
        def _(e):
            run("pool", e)

        @block.sync
        def _(e):
            run("sp", e)


ARENA_BYTES = 212000
OFF_A = 25 * 1024
OFF_B = 65 * 1024
OFF_C = 129 * 1024


class Builder:
    def __init__(self, layers, final_norm, dbg=()):
        self.layers = list(layers)
        self.final_norm = final_norm
        self.dbg = set(dbg)
        self.nc = bass.Bass("TRN2", target_bir_lowering=False)
        self.P = Plan()
        self.out_ops = []

    def dram(self, name, shape, dt, kind="Internal"):
        if kind == "Internal" and name in self.dbg:
            kind = "ExternalOutput"
        return self.nc.dram_tensor(name, list(shape), dt, kind=kind).ap()

    def carve(self, off, shape, dt):
        n = int(np.prod(shape[1:])) * mybir.dt.size(dt)
        off = (off + 31) // 32 * 32
        assert off + n <= ARENA_BYTES, (off, n)
        ap = self.arena[0:shape[0], off:off + n].bitcast(dt)
        if len(shape) == 3:
            ap = ap.rearrange("p (a b) -> p a b", b=shape[2])
        elif len(shape) == 4:
            ap = ap.rearrange("p (a b c) -> p a b c", b=shape[2], c=shape[3])
        return ap, off + n

    class Bump:
        def __init__(self, b, off, limit=ARENA_BYTES):
            self.b, self.off, self.limit = b, off, limit

        def __call__(self, shape, dt):
            ap, self.off = self.b.carve(self.off, shape, dt)
            assert self.off <= self.limit, (self.off, self.limit)
            return ap

    def mm(self, out, lhsT, rhs, start, stop, r, w):
        return self.P.add("pe", lambda e: e.matmul(out, lhsT=lhsT, rhs=rhs, start=start, stop=stop), r, w)

    def tr(self, out, in_, r, w):
        ident = self.ident[0:in_.shape[0], 0:in_.shape[0]]
        return self.P.add("pe", lambda e: e.transpose(out, in_, ident), r, w)

    def act(self, out, in_, func, r, w, **kw):
        return self.P.add("act", lambda e: e.activation(out=out, in_=in_, func=func, **kw), r, w)

    def op(self, eng, name, r, w, **kw):
        return self.P.add(eng, lambda e: getattr(e, name)(**kw), r, w)

    def ld(self, out, in_, r, w, dkey, eng="sp"):
        if len(out.shape) == 3 and out.shape[1] > 4 and len(in_.shape) == 3:
            pieces = []
            for i in range(0, out.shape[1], 4):
                o_, i_ = out[:, i:i + 4, :], in_[:, i:i + 4, :]
                pieces.append(self.P.dma(eng, (lambda o_, i_: (lambda e: e.dma_start(out=o_, in_=i_)))(o_, i_), r, w, dkey=dkey,
                                         exclude=tuple(pieces)))
            return pieces[-1]
        return self.P.dma(eng, lambda e: e.dma_start(out=out, in_=in_), r, w, dkey=dkey)

    def build(self):
        nc = self.nc
        L = len(self.layers)
        ein = lambda n, sh, dt=F32: nc.dram_tensor(n, list(sh), dt, kind="ExternalInput").ap()
        self.xin = ein("xin", [S, D])
        self.pos = ein("pos", [128, NS], I32)
        self.pin = ein("pin", [DEPTH, S, 256])
        self.w = {}
        for n, sh in [("g_mix", [DEPTH, D]), ("w_in", [DEPTH, D, INC]), ("lambda_q1", [DEPTH, 64]), ("lambda_k1", [DEPTH, 64]),
                      ("lambda_q2", [DEPTH, 64]), ("lambda_k2", [DEPTH, 64]), ("g_subln", [DEPTH, 128]), ("g_q_latent", [DEPTH, 512]),
                      ("w_q_up", [DEPTH, 512, 1536]), ("g_kv_latent", [DEPTH, 256]), ("w_kv_up", [DEPTH, 256, 2048]),
                      ("w_branch_diff", [DEPTH, 1024, D]), ("w_branch_mla", [DEPTH, 1024, D]), ("w_out", [DEPTH, D, D]),
                      ("g_ffn", [DEPTH, D]), ("w_gate_up", [DEPTH, D, 2 * DFF]), ("w_down", [DEPTH, DFF, D]),
                      ("w_ple_in", [DEPTH, 256, D]), ("g_ple", [DEPTH, D]), ("w_ple_gate", [DEPTH, D, D]), ("g_final", [D])]:
            self.w[n] = ein(n, sh)
        self.out = nc.dram_tensor("out", [T, D], F32, kind="ExternalOutput").ap()
        self.xres = self.dram("xres", [S, D], F32)
        self.QT_d = self.dram("QT_d", [8, 128, T], BF16)
        self.KT_d = self.dram("KT_d", [8, 128, S], BF16)
        self.V_d = self.dram("V_d", [S, 1024], BF16)
        self.QnT_d = self.dram("QnT_d", [8, 128, T], BF16)
        self.QrT_d = self.dram("QrT_d", [4, 128, T], BF16)
        self.KnT_d = self.dram("KnT_d", [8, 128, S], BF16)
        self.Vm_d = self.dram("Vm_d", [S, 1024], BF16)
        self.GT_d = self.dram("GT_d", [2, KC, 128, T], F32)
        self.actT_d = self.dram("actT_d", [NT, 128, FC, 128], BF16)
        self.dbg_o = {}
        for n, sh, dt in [("dbg_hT", [128, KC, T], BF16), ("dbg_cqT", [128, 4, T], BF16), ("dbg_ckvT", [128, 2, S], BF16),
                          ("dbg_KrT2", [128, S], BF16), ("dbg_odT", [128, 8, T], BF16), ("dbg_omT", [128, 8, T], BF16),
                          ("dbg_mT", [128, KC, T], BF16), ("dbg_tab", [128, NS, 64], F32)]:
            if n in self.dbg:
                self.dbg_o[n] = nc.dram_tensor(n, sh, dt, kind="ExternalOutput").ap()

        with ExitStack() as es:
            self.arena = es.enter_context(nc.sbuf_tensor("arena", [128, ARENA_BYTES], U8))
            self.psum = [es.enter_context(nc.psum_tensor(f"ps{i}", [128, 512], F32))[:] for i in range(8)]
            pb = self.Bump(self, 0, OFF_A)
            self.ident = pb([128, 128], BF16)
            self.ones_bf = pb([128, 128], BF16)
            self.ones_f = pb([128, 128], F32)
            self.CCp = pb([128, NS, 16], F32)
            self.SSp = pb([128, NS, 16], F32)
            self.CCm = pb([128, NS, 64], F32)
            self.SSm = pb([128, NS, 64], F32)
            self.gq_b = pb([128, 512], F32)
            self.gkv_b = pb([128, 256], F32)
            self.gs_sub = pb([128, 1], F32)
            self.lamt = pb([128, 8], F32)
            self.persist_end = pb.off

            self.phase_consts()
            for (layer, src_name, own0, oth0) in self.layers:
                self.own0, self.oth0 = own0, oth0
                self.src = self.xin if src_name == "xin" else self.xres
                self.layer(layer)
            if self.final_norm:
                self.phase_final()
            else:
                self.phase_copy_out()
            self.P.emit(nc, es, final_wait_ops=self.out_ops)
        return nc

    def phase_consts(self):
        P = self.P
        b = self.Bump(self, OFF_C)
        identf = b([128, 128], F32)
        self.op("pool", "memset", [], ["identf"], ap=identf, constant=0.0)
        P.add("pool", lambda e: e.affine_select(out=identf, in_=identf, compare_op=ALU.not_equal, fill=1.0, base=0,
                                                 pattern=[[-1, 128]], channel_multiplier=1), ["identf"], ["identf"])
        self.op("pool", "tensor_copy", ["identf"], ["ident"], out=self.ident, in_=identf)
        self.op("pool", "memset", [], ["ones_bf"], ap=self.ones_bf, constant=1.0)
        self.op("pool", "memset", [], ["ones_f"], ap=self.ones_f, constant=1.0 / 128.0)
        post = b([128, NS], I32)
        posf = b([128, NS], F32)
        self.ld(post, self.pos, [], ["post"], dkey="c_pos")
        self.op("dve", "tensor_copy", ["post"], ["posf"], out=posf, in_=post)
        for nm, nf, CC, SS in (("p", 8, self.CCp, self.SSp), ("m", 32, self.CCm, self.SSm)):
            dim = 2 * nf
            invf = b([128, nf], F32)
            for j in range(nf):
                self.op("dve", "memset", [], ["invf" + nm], ap=invf[:, j:j + 1], constant=float(np.float32(ROPE_THETA) ** np.float32(-(2 * j) / dim)))
            sh = [128, NS, nf]
            ang = b(sh, F32); uu = b(sh, F32); ki = b(sh, I32); kf = b(sh, F32); r1 = b(sh, F32); r2 = b(sh, F32)
            mm_ = b(sh, F32); rs = b(sh, F32); rc = b(sh, F32); sn = b(sh, F32); cs = b(sh, F32)
            k = lambda s_: s_ + nm
            self.op("dve", "tensor_tensor", ["posf", k("invf")], [k("ang")], out=ang, in0=posf.unsqueeze(2).broadcast_to(sh),
                    in1=invf.unsqueeze(1).broadcast_to(sh), op=ALU.mult)
            self.op("dve", "tensor_scalar", [k("ang")], [k("uu")], out=uu, in0=ang, scalar1=float(1 / (2 * np.pi)), scalar2=None, op0=ALU.mult)
            self.op("dve", "tensor_copy", [k("uu")], [k("ki")], out=ki, in_=uu)
            self.op("dve", "tensor_copy", [k("ki")], [k("kf")], out=kf, in_=ki)
            C1 = 6.28125
            C2 = float(2 * np.pi - 6.28125)
            self.op("dve", "scalar_tensor_tensor", [k("kf"), k("ang")], [k("r1")], out=r1, in0=kf, scalar=-C1, in1=ang, op0=ALU.mult, op1=ALU.add)
            self.op("dve", "scalar_tensor_tensor", [k("kf"), k("r1")], [k("r2")], out=r2, in0=kf, scalar=-C2, in1=r1, op0=ALU.mult, op1=ALU.add)

            def wrap(dst, src, shift, kd, ks):
                self.op("dve", "tensor_scalar", [ks], [k("mm")], out=mm_, in0=src, scalar1=float(PI - shift), scalar2=float(-2 * PI), op0=ALU.is_ge, op1=ALU.mult)
                self.op("dve", "scalar_tensor_tensor", [ks, k("mm")], [kd], out=dst, in0=src, scalar=float(shift), in1=mm_, op0=ALU.add, op1=ALU.add)
                self.op("dve", "tensor_scalar", [kd], [kd], out=dst, in0=dst, scalar1=-PI, scalar2=PI, op0=ALU.max, op1=ALU.min)

            wrap(rs, r2, 0.0, k("rs"), k("r2"))
            wrap(rc, rs, PI / 2, k("rc"), k("rs"))
            self.act(sn, rs, AF.Sin, [k("rs")], [k("sn")])
            self.act(cs, rc, AF.Sin, [k("rc")], [k("cs")])
            tk = "tab" + nm
            self.op("dve", "tensor_copy", [k("cs")], [tk + "c0"], out=CC[:, :, 0:nf], in_=cs)
            self.op("dve", "tensor_copy", [k("cs")], [tk + "c1"], out=CC[:, :, nf:dim], in_=cs)
            self.op("dve", "tensor_scalar", [k("sn")], [tk + "s0"], out=SS[:, :, 0:nf], in0=sn, scalar1=-1.0, scalar2=None, op0=ALU.mult)
            self.op("dve", "tensor_copy", [k("sn")], [tk + "s1"], out=SS[:, :, nf:dim], in_=sn)
        if "dbg_tab" in self.dbg_o:
            self.out_ops.append(self.ld(self.dbg_o["dbg_tab"], self.CCm, ["tabmc0", "tabmc1"], [], dkey="dbg"))
        P.barrier()

    def rope(self, b_tmp, ps3, out3, CC, SS, rd, n, rk, wk, tag):
        t1, t2 = b_tmp
        hf = rd // 2
        shp = [128, n, rd]
        bc = lambda a: a.unsqueeze(1).broadcast_to([128, n, a.shape[-1]])
        self.op("dve", "tensor_tensor", rk, [tag + "t1"], out=t1, in0=ps3[:, :, 0:rd], in1=bc(CC), op=ALU.mult)
        self.op("dve", "tensor_tensor", rk, [tag + "t2a"], out=t2[:, :, 0:hf], in0=ps3[:, :, hf:rd], in1=bc(SS[:, 0:hf]), op=ALU.mult)
        self.op("dve", "tensor_tensor", rk, [tag + "t2b"], out=t2[:, :, hf:rd], in0=ps3[:, :, 0:hf], in1=bc(SS[:, hf:rd]), op=ALU.mult)
        return self.op("dve", "tensor_tensor", [tag + "t1", tag + "t2a", tag + "t2b"], wk, out=out3[:, :, 0:rd], in0=t1, in1=t2, op=ALU.add)

    def norm_pass(self, b, x_src, row0, ntiles, g_ap, hT, hT_key, tag):
        g_b = b([128, D], F32)
        xt = b([128, D], F32)
        hb = [b([128, D], BF16) for _ in range(2)]
        ss = [b([128, 1], F32) for _ in range(2)]
        self.ld(g_b, g_ap.partition_broadcast(128), [], [tag + "g_b"], dkey="n_g")
        for tt in range(ntiles):
            sl = tt % 2
            self.ld(xt, x_src[row0 + tt * 128: row0 + (tt + 1) * 128, :], [x_src.tensor.name], [tag + "xt"], dkey="n_xt")
            self.act(hb[sl], xt, AF.Square, [tag + "xt"], [tag + f"hb{sl}", tag + f"ss{sl}"], accum_out=ss[sl])
            self.act(ss[sl], ss[sl], AF.Sqrt, [tag + f"ss{sl}"], [tag + f"ss{sl}"], bias=1e-6, scale=1.0 / D)
            self.op("dve", "reciprocal", [tag + f"ss{sl}"], [tag + f"ss{sl}"], out=ss[sl], in_=ss[sl])
            self.op("dve", "scalar_tensor_tensor", [tag + "xt", tag + f"ss{sl}", tag + "g_b"], [tag + f"hb{sl}"], out=hb[sl], in0=xt,
                    scalar=ss[sl][:, 0:1], in1=g_b, op0=ALU.mult, op1=ALU.mult)
            for half in range(2):
                bank = 4 + (tt * 2 + half) % 2
                pt = self.psum[bank].bitcast(BF16).rearrange("p (a b) -> p a b", b=128)
                for j in range(8):
                    kc = half * 8 + j
                    self.tr(pt[:, j, :], hb[sl][:, kc * 128:(kc + 1) * 128], [tag + f"hb{sl}"], [f"ps{bank}"])
                dst = hT[:, half * 8:(half + 1) * 8, tt * 128:(tt + 1) * 128]
                if half == 0:
                    self.act(dst, pt, AF.Copy, [f"ps{bank}"], [hT_key])
                else:
                    self.op("dve", "tensor_copy", [f"ps{bank}"], [hT_key], out=dst, in_=pt)

    def layer(self, l):
        stop = os.environ.get("KSTOP", "")
        self.phase_z(l)
        if stop == "z":
            return
        self.phase_up(l)
        if stop == "up":
            return
        self.phase_attn(l)
        if stop == "attn":
            return
        self.phase_branch(l)
        self.phase_wout(l)
        if stop == "wout":
            return
        self.phase_ffn(l)
        if stop == "ffn":
            return
        self.phase_ple(l)

    def evac(self, i, out, in_, r, w):
        if i % 2 == 0:
            return self.act(out, in_, AF.Copy, r, w)
        return self.op("dve", "tensor_copy", r, w, out=out, in_=in_)

    def phase_up(self, l):
        P = self.P
        b = self.Bump(self, OFF_C)
        wq = b([128, 4, 1536], BF16)
        wkv = b([128, 2, 2048], BF16)
        self.ld(wq, self.w["w_q_up"][l].rearrange("(kc p) n -> p kc n", p=128), [], ["wq"], dkey="ws0", eng="pool")
        self.ld(wkv, self.w["w_kv_up"][l].rearrange("(kc p) n -> p kc n", p=128), [], ["wkv"], dkey="ws1", eng="pool")
        st = [b([128, 512], BF16) for _ in range(2)]
        rbm = [b([128, 512], BF16) for _ in range(2)]
        t1 = b([128, 8, 64], F32); t2 = b([128, 8, 64], F32)
        n = {"ps": 0, "st": 0, "rb": 0, "tp": 0}

        def nxt(k, m):
            v = n[k] % m; n[k] += 1
            return v
        wq3 = wq.rearrange("p k (h c) -> p k h c", c=192)
        wkv3 = wkv.rearrange("p k (h c) -> p k h c", c=256)
        for h in range(8):
            for tb in range(T // 512):
                bank = nxt("ps", 4); ps = self.psum[bank]
                for kc in range(4):
                    self.mm(ps, wq3[:, kc, h, 0:128], self.cqT[:, kc, tb * 512:(tb + 1) * 512], kc == 0, kc == 3, ["wq", "cqT"], [f"ps{bank}"])
                sl = nxt("st", 2)
                self.evac(sl, st[sl], ps, [f"ps{bank}"], [f"st{sl}"])
                self.ld(self.QnT_d[h, :, tb * 512:(tb + 1) * 512], st[sl], [f"st{sl}"], ["QnT_d"], dkey=f"st{sl}")
        for tt in range(NT):
            bank = nxt("ps", 4); ps = self.psum[bank]
            ps3 = ps.rearrange("p (h r) -> p h r", r=64)
            for kc in range(4):
                self.mm(ps3, self.cqT[:, kc, tt * 128:(tt + 1) * 128], wq3[:, kc, :, 128:192], kc == 0, kc == 3, ["wq", "cqT"], [f"ps{bank}"])
            r = nxt("rb", 2)
            rb3 = rbm[r].rearrange("p (h r) -> p h r", r=64)
            self.rope((t1, t2), ps3, rb3, self.CCm[:, self.own0 // 128 + tt, :], self.SSm[:, self.own0 // 128 + tt, :], 64, 8, [f"ps{bank}"], [f"rbm{r}"], "rq")
            tbank = 6 + nxt("tp", 2)
            pt = self.psum[tbank].bitcast(BF16)[:, 0:512].rearrange("p (a b) -> p a b", b=128)
            for j in range(4):
                self.tr(pt[:, j, :], rbm[r][:, j * 128:(j + 1) * 128], [f"rbm{r}"], [f"ps{tbank}"])
            sl = nxt("st", 2)
            s3 = st[sl].rearrange("p (a b) -> p a b", b=128)
            self.evac(sl, s3, pt, [f"ps{tbank}"], [f"st{sl}"])
            self.ld(self.QrT_d[:, :, tt * 128:(tt + 1) * 128].rearrange("h p t -> p h t"), s3, [f"st{sl}"], ["QrT_d"], dkey=f"st{sl}")
        for h in range(8):
            for tb in range(S // 512):
                bank = nxt("ps", 4); ps = self.psum[bank]
                for kc in range(2):
                    self.mm(ps, wkv3[:, kc, h, 0:128], self.ckvT[:, kc, tb * 512:(tb + 1) * 512], kc == 0, kc == 1, ["wkv", "ckvT"], [f"ps{bank}"])
                sl = nxt("st", 2)
                self.evac(sl, st[sl], ps, [f"ps{bank}"], [f"st{sl}"])
                self.ld(self.KnT_d[h, :, tb * 512:(tb + 1) * 512], st[sl], [f"st{sl}"], ["KnT_d"], dkey=f"st{sl}")
        for gt in range(NS):
            for half in range(2):
                bank = nxt("ps", 4); ps = self.psum[bank]
                ps3 = ps.rearrange("p (h r) -> p h r", r=128)
                for kc in range(2):
                    self.mm(ps3, self.ckvT[:, kc, gt * 128:(gt + 1) * 128], wkv3[:, kc, half * 4:(half + 1) * 4, 128:256], kc == 0, kc == 1,
                            ["wkv", "ckvT"], [f"ps{bank}"])
                sl = nxt("st", 2)
                self.evac(sl, st[sl], ps, [f"ps{bank}"], [f"st{sl}"])
                self.ld(self.Vm_d[gt * 128:(gt + 1) * 128, half * 512:(half + 1) * 512], st[sl], [f"st{sl}"], ["Vm_d"], dkey=f"st{sl}")
        P.barrier()

    def phase_attn(self, l):
        P = self.P
        lam_init = 0.8 - 0.6 * math.exp(-0.3 * l)
        self.odT, _ = self.carve(OFF_B, [128, 8, T], BF16)
        self.omT, _ = self.carve(OFF_B + 32 * 1024, [128, 8, T], BF16)
        a = self.Bump(self, OFF_A, OFF_A + 32 * 1024)
        lq = [a([128, 64], F32) for _ in range(4)]
        lt = a([128, 64], F32)
        ls = [a([128, 1], F32) for _ in range(2)]
        nlam = a([128, 1], F32)
        gsub = a([128, 1], F32)
        r1 = a([128, 512], F32); r2 = a([128, 512], F32); Av = a([128, 512], F32); Bv = a([128, 512], F32)
        sq = a([128, 512], F32); rstd = a([128, 512], F32)
        for i, nme in enumerate(("lambda_q1", "lambda_k1", "lambda_q2", "lambda_k2")):
            self.ld(lq[i], self.w[nme][l].partition_broadcast(128), [], [f"lq{i}"], dkey=f"c_l{i}")
        self.ld(gsub, self.w["g_subln"][l].rearrange("(p o) -> p o", o=1), [], ["gsub"], dkey="c_gq")
        for j in range(2):
            self.op("dve", "tensor_tensor", [f"lq{2 * j}", f"lq{2 * j + 1}"], ["lt"], out=lt, in0=lq[2 * j], in1=lq[2 * j + 1], op=ALU.mult)
            self.op("dve", "reduce_sum", ["lt"], [f"ls{j}"], out=ls[j], in_=lt, axis=mybir.AxisListType.X)
            self.act(ls[j], ls[j], AF.Exp, [f"ls{j}"], [f"ls{j}"])
        self.op("dve", "tensor_tensor", ["ls0", "ls1"], ["nlam"], out=nlam, in0=ls[1], in1=ls[0], op=ALU.subtract)
        self.op("dve", "tensor_scalar", ["nlam"], ["nlam"], out=nlam, in0=nlam, scalar1=float(-lam_init), scalar2=None, op0=ALU.add)
        self.op("dve", "tensor_scalar", ["gsub"], ["gsub"], out=gsub, in0=gsub, scalar1=float(1.0 - lam_init), scalar2=None, op0=ALU.mult)

        b = self.Bump(self, OFF_C)
        Qb = [b([128, T], BF16) for _ in range(2)]
        Q2b = [b([128, T], BF16) for _ in range(2)]
        Kb = [b([128, S], BF16) for _ in range(2)]
        Vb = [b([128, NS, 128], BF16) for _ in range(2)]
        Pt = [b([128, 512], BF16) for _ in range(6)]
        n = {"s": 0, "p": 0}

        def nxt(k, m):
            v = n[k] % m; n[k] += 1
            return v
        sc_d = 64 ** -0.5
        sc_m = 192 ** -0.5
        for h in range(8):
            hb = h % 2
            self.ld(Qb[hb], self.QT_d[h], ["QT_d"], [f"Q{hb}"], dkey=f"aq{hb}")
            self.ld(Kb[hb], self.KT_d[h], ["KT_d"], [f"K{hb}"], dkey=f"ak{hb}")
            self.ld(Vb[hb], self.V_d[:, h * 128:(h + 1) * 128].rearrange("(kt p) e -> p kt e", p=128), ["V_d"], [f"V{hb}"], dkey=f"av{hb}")
            for qb in range(T // 512):
                qs = slice(qb * 512, (qb + 1) * 512)
                O1, O2, Z1, Z2 = self.psum[4], self.psum[5], self.psum[6], self.psum[7]
                pend = None
                for kt in range(NS + 1):
                    cur = None
                    if kt < NS:
                        ks = slice(kt * 128, (kt + 1) * 128)
                        pts = []
                        for c in range(2):
                            bank = nxt("s", 4); ps = self.psum[bank]
                            self.mm(ps, Kb[hb][c * 64:(c + 1) * 64, ks], Qb[hb][c * 64:(c + 1) * 64, qs], True, True, [f"K{hb}", f"Q{hb}"], [f"ps{bank}"])
                            p = nxt("p", 6)
                            self.act(Pt[p], ps, AF.Exp, [f"ps{bank}"], [f"Pt{p}"], scale=sc_d)
                            pts.append(p)
                        cur = (kt, pts)
                    if pend is not None:
                        kp, pts = pend
                        for c, (O, Z) in enumerate(((O1, Z1), (O2, Z2))):
                            p = pts[c]
                            self.mm(O, Vb[hb][:, kp, :], Pt[p], kp == 0, kp == NS - 1, [f"V{hb}", f"Pt{p}"], [f"ps{4 + c}"])
                            self.mm(Z, self.ones_bf, Pt[p], kp == 0, kp == NS - 1, ["ones_bf", f"Pt{p}"], [f"ps{6 + c}"])
                    pend = cur
                self.op("dve", "reciprocal", ["ps6"], ["r1"], out=r1, in_=Z1)
                self.op("dve", "reciprocal", ["ps7"], ["r2"], out=r2, in_=Z2)
                self.op("dve", "tensor_tensor", ["ps4", "r1"], ["Av"], out=Av, in0=O1, in1=r1, op=ALU.mult)
                self.op("dve", "tensor_tensor", ["ps5", "r2"], ["Bv"], out=Bv, in0=O2, in1=r2, op=ALU.mult)
                self.op("dve", "scalar_tensor_tensor", ["Bv", "Av", "nlam"], ["Av2"], out=Av, in0=Bv, scalar=nlam[:, 0:1], in1=Av, op0=ALU.mult, op1=ALU.add)
                self.op("pool", "tensor_tensor", ["Av2"], ["sq"], out=sq, in0=Av, in1=Av, op=ALU.mult)
                bank = nxt("s", 4); ms = self.psum[bank]
                self.mm(ms, self.ones_f, sq, True, True, ["ones_f", "sq"], [f"ps{bank}"])
                self.act(rstd, ms, AF.Sqrt, [f"ps{bank}"], ["rstd"], bias=1e-5, scale=1.0)
                self.op("dve", "reciprocal", ["rstd"], ["rstd"], out=rstd, in_=rstd)
                self.op("dve", "scalar_tensor_tensor", ["Av2", "gsub", "rstd"], ["odT"], out=self.odT[:, h, qs], in0=Av, scalar=gsub[:, 0:1], in1=rstd,
                        op0=ALU.mult, op1=ALU.mult)
        for h in range(8):
            hb = h % 2
            p0 = 64 * (h % 2)
            self.ld(Qb[hb], self.QnT_d[h], ["QnT_d"], [f"Q{hb}"], dkey=f"aq{hb}")
            self.ld(Q2b[hb], self.QrT_d[h // 2], ["QrT_d"], [f"Q2{hb}"], dkey=f"aq2{hb}")
            self.ld(Kb[hb], self.KnT_d[h], ["KnT_d"], [f"K{hb}"], dkey=f"ak{hb}")
            self.ld(Vb[hb], self.Vm_d[:, h * 128:(h + 1) * 128].rearrange("(kt p) e -> p kt e", p=128), ["Vm_d"], [f"V{hb}"], dkey=f"av{hb}")
            for qb in range(T // 512):
                qs = slice(qb * 512, (qb + 1) * 512)
                ob = 4 + 2 * (qb % 2)
                O, Z = self.psum[ob], self.psum[ob + 1]
                pend = None
                for kt in range(NS + 1):
                    cur = None
                    if kt < NS:
                        ks = slice(kt * 128, (kt + 1) * 128)
                        bank = nxt("s", 4); ps = self.psum[bank]
                        self.mm(ps, Kb[hb][:, ks], Qb[hb][:, qs], True, False, [f"K{hb}", f"Q{hb}"], [f"ps{bank}"])
                        self.mm(ps, self.KrT2[p0:p0 + 64, ks], Q2b[hb][p0:p0 + 64, qs], False, True, ["KrT2", f"Q2{hb}"], [f"ps{bank}"])
                        p = nxt("p", 6)
                        self.act(Pt[p], ps, AF.Exp, [f"ps{bank}"], [f"Pt{p}"], scale=sc_m)
                        cur = (kt, p)
                    if pend is not None:
                        kp, p = pend
                        self.mm(O, Vb[hb][:, kp, :], Pt[p], kp == 0, kp == NS - 1, [f"V{hb}", f"Pt{p}"], [f"ps{ob}"])
                        self.mm(Z, self.ones_bf, Pt[p], kp == 0, kp == NS - 1, ["ones_bf", f"Pt{p}"], [f"ps{ob + 1}"])
                    pend = cur
                self.op("dve", "reciprocal", [f"ps{ob + 1}"], ["r1"], out=r1, in_=Z)
                self.op("dve", "tensor_tensor", [f"ps{ob}", "r1"], ["omT"], out=self.omT[:, h, qs], in0=O, in1=r1, op=ALU.mult)
        for nme, t_, k_ in (("dbg_odT", self.odT, "odT"), ("dbg_omT", self.omT, "omT")):
            if nme in self.dbg_o:
                self.out_ops.append(self.ld(self.dbg_o[nme], t_, [k_], [], dkey="dbg"))
        P.barrier()

    def phase_branch(self, l):
        P = self.P
        self.mT, _ = self.carve(OFF_C, [128, KC, T], BF16)
        a = self.Bump(self, OFF_A, OFF_B)
        wbd = [a([128, 8, 128], BF16) for _ in range(2)]
        wbm = [a([128, 8, 128], BF16) for _ in range(2)]
        sga = [a([128, 512], F32) for _ in range(2)]
        sgb = [a([128, 512], F32) for _ in range(2)]
        m1 = [a([128, 512], F32) for _ in range(2)]
        m2 = [a([128, 512], F32) for _ in range(2)]
        it = 0
        for cc in range(KC):
            ws = cc % 2
            cs = slice(cc * 128, (cc + 1) * 128)
            self.ld(wbd[ws], self.w["w_branch_diff"][l][:, cs].rearrange("(h p) n -> p h n", p=128), [], [f"wbd{ws}"], dkey=f"ws{ws}", eng="pool")
            self.ld(wbm[ws], self.w["w_branch_mla"][l][:, cs].rearrange("(h p) n -> p h n", p=128), [], [f"wbm{ws}"], dkey=f"wsb{ws}", eng="pool")
            for tb in range(T // 512):
                ts_ = slice(tb * 512, (tb + 1) * 512)
                g = it % 2; it += 1
                bd, bm = 2 * g, 2 * g + 1
                self.ld(sga[g], self.GT_d[0, cc, :, ts_], ["GT_d"], [f"sga{g}"], dkey=f"ga{g}")
                self.ld(sgb[g], self.GT_d[1, cc, :, ts_], ["GT_d"], [f"sgb{g}"], dkey=f"gb{g}")
                for h in range(8):
                    self.mm(self.psum[bd], wbd[ws][:, h, :], self.odT[:, h, ts_], h == 0, h == 7, [f"wbd{ws}", "odT"], [f"ps{bd}"])
                for h in range(8):
                    self.mm(self.psum[bm], wbm[ws][:, h, :], self.omT[:, h, ts_], h == 0, h == 7, [f"wbm{ws}", "omT"], [f"ps{bm}"])
                self.op("dve", "tensor_tensor", [f"ps{bd}", f"sga{g}"], [f"m1{g}"], out=m1[g], in0=self.psum[bd], in1=sga[g], op=ALU.mult)
                self.op("dve", "tensor_tensor", [f"ps{bm}", f"sgb{g}"], [f"m2{g}"], out=m2[g], in0=self.psum[bm], in1=sgb[g], op=ALU.mult)
                self.op("pool", "tensor_tensor", [f"m1{g}", f"m2{g}"], ["mT"], out=self.mT[:, cc, ts_], in0=m1[g], in1=m2[g], op=ALU.add)
        if "dbg_mT" in self.dbg_o:
            self.out_ops.append(self.ld(self.dbg_o["dbg_mT"], self.mT, ["mT"], [], dkey="dbg"))
        P.barrier()

    def gemm_resid(self, lhsT_fn, nk, w_ap, w_pat, x_src, src0, tagk, pre=None):
        a = self.Bump(self, OFF_A, OFF_C)
        ws = [a([128, nk, 512], BF16) for _ in range(2)]
        xo = [a([128, 512], F32) for _ in range(2)]
        xn = [a([128, 512], F32) for _ in range(2)]
        it = 0
        for cg in range(D // 512):
            sl = cg % 2
            cs = slice(cg * 512, (cg + 1) * 512)
            self.ld(ws[sl], w_ap[:, cs].rearrange(w_pat, p=128), [], [f"ws{sl}"], dkey=f"ws{sl}", eng="pool")
            for tt in range(NT):
                rs = slice(tt * 128, (tt + 1) * 128)
                g = it % 2; it += 1
                bank = it % 4
                self.ld(xo[g], x_src[src0 + tt * 128:src0 + (tt + 1) * 128, cs], [x_src.tensor.name], [f"xo{g}"], dkey=f"xo{g}")
                for k in range(nk):
                    lt, rk = lhsT_fn(tt, k)
                    self.mm(self.psum[bank], lt, ws[sl][:, k, :], k == 0, k == nk - 1, [f"ws{sl}"] + rk, [f"ps{bank}"])
                self.op("dve", "tensor_tensor", [f"ps{bank}", f"xo{g}"], [f"xn{g}"], out=xn[g], in0=self.psum[bank], in1=xo[g], op=ALU.add)
                self.ld(self.xres[self.own0 + tt * 128:self.own0 + (tt + 1) * 128, cs], xn[g], [f"xn{g}"], ["xres"], dkey=f"xn{g}")

    def phase_wout(self, l):
        self.gemm_resid(lambda tt, k: (self.mT[:, k, tt * 128:(tt + 1) * 128], ["mT"]), KC, self.w["w_out"][l], "(kc p) n -> p kc n", self.src, self.own0, "wo")
        self.P.barrier()

    def phase_ffn(self, l):
        P = self.P
        hT, _ = self.carve(OFF_B, [128, KC, T], BF16)
        b = self.Bump(self, OFF_C)
        self.norm_pass(b, self.xres, self.own0, NT, self.w["g_ffn"][l], hT, "hT", f"f{l}")
        wg = [b([128, KC, 512], BF16) for _ in range(1)]
        wu = [b([128, KC, 512], BF16) for _ in range(1)]
        a = self.Bump(self, OFF_A, OFF_B)
        wg.append(a([128, KC, 512], BF16)); wu.append(a([128, KC, 512], BF16))
        sg = [a([128, 512], F32) for _ in range(2)]
        av = [a([128, 512], BF16) for _ in range(2)]
        Wgu = self.w["w_gate_up"][l]
        it = 0
        for fg in range(DFF // 512):
            sl = fg % 2
            self.ld(wg[sl], Wgu[:, fg * 512:(fg + 1) * 512].rearrange("(kc p) n -> p kc n", p=128), [], [f"wg{sl}"], dkey=f"ws{sl}", eng="pool")
            self.ld(wu[sl], Wgu[:, DFF + fg * 512:DFF + (fg + 1) * 512].rearrange("(kc p) n -> p kc n", p=128), [], [f"wu{sl}"], dkey=f"wsb{sl}", eng="pool")
            for j in range(4):
                fc = fg * 4 + j
                for tb in range(T // 512):
                    ts_ = slice(tb * 512, (tb + 1) * 512)
                    g = it % 2; it += 1
                    bg, bu = 2 * g, 2 * g + 1
                    for kc in range(KC):
                        self.mm(self.psum[bg], wg[sl][:, kc, j * 128:(j + 1) * 128], hT[:, kc, ts_], kc == 0, kc == KC - 1, [f"wg{sl}", "hT"], [f"ps{bg}"])
                    for kc in range(KC):
                        self.mm(self.psum[bu], wu[sl][:, kc, j * 128:(j + 1) * 128], hT[:, kc, ts_], kc == 0, kc == KC - 1, [f"wu{sl}", "hT"], [f"ps{bu}"])
                    self.act(sg[g], self.psum[bg], AF.Silu, [f"ps{bg}"], [f"sg{g}"])
                    self.op("dve", "tensor_tensor", [f"sg{g}", f"ps{bu}"], [f"av{g}"], out=av[g], in0=sg[g], in1=self.psum[bu], op=ALU.mult)
                    self.ld(self.actT_d[tb * 4:(tb + 1) * 4, :, fc, :].rearrange("j p t -> p j t"), av[g].rearrange("p (j t) -> p j t", t=128),
                            [f"av{g}"], ["actT_d"], dkey=f"av{g}")
        P.barrier()
        a = self.Bump(self, OFF_A)
        wd = [a([128, FC, 512], BF16) for _ in range(2)]
        at = [a([128, FC, 128], BF16) for _ in range(2)]
        xo = [a([128, 512], F32) for _ in range(2)]
        xn = [a([128, 512], F32) for _ in range(2)]
        it = 0
        for cg in range(D // 512):
            sl = cg % 2
            cs = slice(cg * 512, (cg + 1) * 512)
            self.ld(wd[sl], self.w["w_down"][l][:, cs].rearrange("(fc p) n -> p fc n", p=128), [], [f"wd{sl}"], dkey=f"ws{sl}", eng="pool")
            for tt in range(NT):
                rs = slice(tt * 128, (tt + 1) * 128)
                g = it % 2; it += 1
                bank = it % 4
                xs_ = slice(self.own0 + tt * 128, self.own0 + (tt + 1) * 128)
                self.ld(at[g].rearrange("p f t -> p (f t)"), self.actT_d[tt].rearrange("p f t -> p (f t)"), ["actT_d"], [f"at{g}"], dkey=f"at{g}")
                self.ld(xo[g], self.xres[xs_, cs], ["xres"], [f"xo{g}"], dkey=f"xo{g}")
                for fc in range(FC):
                    self.mm(self.psum[bank], at[g][:, fc, :], wd[sl][:, fc, :], fc == 0, fc == FC - 1, [f"wd{sl}", f"at{g}"], [f"ps{bank}"])
                self.op("dve", "tensor_tensor", [f"ps{bank}", f"xo{g}"], [f"xn{g}"], out=xn[g], in0=self.psum[bank], in1=xo[g], op=ALU.add)
                self.ld(self.xres[xs_, cs], xn[g], [f"xn{g}"], ["xres"], dkey=f"xn{g}")
        P.barrier()

    def phase_ple(self, l):
        P = self.P
        hT, _ = self.carve(OFF_B, [128, KC, T], BF16)
        b = self.Bump(self, OFF_C)
        self.norm_pass(b, self.xres, self.own0, NT, self.w["g_ple"][l], hT, "hT", f"p{l}")
        a = self.Bump(self, OFF_A, OFF_B)
        pT = a([128, 2, T], BF16)
        pf = [a([128, 256], F32) for _ in range(2)]
        pb = [a([128, 256], BF16) for _ in range(2)]
        for tt in range(NT):
            g = tt % 2
            self.ld(pf[g], self.pin[l, self.own0 + tt * 128:self.own0 + (tt + 1) * 128, :], [], [f"pf{g}"], dkey=f"xo{g}")
            self.op("pool", "tensor_copy", [f"pf{g}"], [f"pb{g}"], out=pb[g], in_=pf[g])
            tbank = 6 + tt % 2
            pt = self.psum[tbank].bitcast(BF16)[:, 0:256].rearrange("p (a b) -> p a b", b=128)
            for j in range(2):
                self.tr(pt[:, j, :], pb[g][:, j * 128:(j + 1) * 128], [f"pb{g}"], [f"ps{tbank}"])
            self.evac(tt, pT[:, :, tt * 128:(tt + 1) * 128], pt, [f"ps{tbank}"], ["pT"])
        ws = [b([128, KC, 512], BF16) for _ in range(2)]
        wi = [b([128, 2, 512], BF16) for _ in range(2)]
        sg = [a([128, 512], F32) for _ in range(2)]
        xo = [a([128, 512], F32) for _ in range(2)]
        xn = [a([128, 512], F32) for _ in range(2)]
        it = 0
        for cg in range(D // 512):
            sl = cg % 2
            cs = slice(cg * 512, (cg + 1) * 512)
            self.ld(ws[sl], self.w["w_ple_gate"][l][:, cs].rearrange("(kc p) n -> p kc n", p=128), [], [f"ws{sl}"], dkey=f"ws{sl}", eng="pool")
            self.ld(wi[sl], self.w["w_ple_in"][l][:, cs].rearrange("(kc p) n -> p kc n", p=128), [], [f"wi{sl}"], dkey=f"wsb{sl}", eng="pool")
            for tt in range(NT):
                rs = slice(tt * 128, (tt + 1) * 128)
                g = it % 2; it += 1
                bg, be = 2 * g, 2 * g + 1
                xs_ = slice(self.own0 + tt * 128, self.own0 + (tt + 1) * 128)
                self.ld(xo[g], self.xres[xs_, cs], ["xres"], [f"xo{g}"], dkey=f"xo{g}")
                for kc in range(KC):
                    self.mm(self.psum[bg], hT[:, kc, rs], ws[sl][:, kc, :], kc == 0, kc == KC - 1, [f"ws{sl}", "hT"], [f"ps{bg}"])
                for kc in range(2):
                    self.mm(self.psum[be], pT[:, kc, rs], wi[sl][:, kc, :], kc == 0, kc == 1, [f"wi{sl}", "pT"], [f"ps{be}"])
                self.act(sg[g], self.psum[bg], AF.Sigmoid, [f"ps{bg}"], [f"sg{g}"])
                self.op("dve", "tensor_tensor", [f"sg{g}", f"ps{be}"], [f"sg{g}"], out=sg[g], in0=sg[g], in1=self.psum[be], op=ALU.mult)
                self.op("pool", "tensor_tensor", [f"sg{g}", f"xo{g}"], [f"xn{g}"], out=xn[g], in0=sg[g], in1=xo[g], op=ALU.add)
                self.ld(self.xres[xs_, cs], xn[g], [f"xn{g}"], ["xres"], dkey=f"xn{g}")
        P.barrier()

    def phase_z(self, l):
        P = self.P
        W = self.w["w_in"]
        a = self.Bump(self, OFF_A, OFF_B)
        self.cqT = a([128, 4, T], BF16)
        self.ckvT = a([128, 2, S], BF16)
        self.KrT2 = a([128, S], BF16)
        self.ld(self.gq_b, self.w["g_q_latent"][l].partition_broadcast(128), [], ["gq_b"], dkey="c_gq")
        self.ld(self.gkv_b, self.w["g_kv_latent"][l].partition_broadcast(128), [], ["gkv_b"], dkey="c_gkv")
        for ps_ in range(2):
            tok0 = ps_ * T
            hT, _ = self.carve(OFF_B, [128, KC, T], BF16)
            b = self.Bump(self, OFF_C)
            tag = f"z{l}{ps_}"
            self.norm_pass(b, self.src, (self.own0 if ps_ == 0 else self.oth0), NT, self.w["g_mix"][l], hT, "hT", tag)
            if "dbg_hT" in self.dbg_o and ps_ == 0 and l == 0 and self.own0 == 0:
                self.out_ops.append(self.ld(self.dbg_o["dbg_hT"], hT, ["hT"], [], dkey="dbg"))
            ws = [b([128, KC, 512], BF16) for _ in range(2)]
            rb = [b([128, 512], BF16) for _ in range(2)]
            t1 = b([128, 8, 16], F32); t2 = b([128, 8, 16], F32)
            t1m = b([128, 1, 64], F32); t2m = b([128, 1, 64], F32)
            qts = [b([128, 512], BF16) for _ in range(2)]
            gst = [b([128, 512], F32) for _ in range(2)]
            cqf = b([128, 512], F32)
            cqs = b([128, 1], F32)
            cqj = b([128, 512], BF16)
            krb = b([128, 128], BF16)
            groups = []
            if ps_ == 0:
                groups += [("dq", C_DQ, 512), ("dq", C_DQ + 512, 512)]
            groups += [("dk", C_DK, 512), ("dk", C_DK + 512, 512), ("dv", C_DV, 512), ("dv", C_DV + 512, 512)]
            if ps_ == 0:
                groups += [("cq", C_CQ, 512)]
            groups += [("ckv", C_CKV, 320)]
            if ps_ == 0:
                groups += [("g", C_GA + i * 512, 512) for i in range(8)]
            cnt = {"ps": 0, "rb": 0, "qts": 0, "gst": 0}
            for gi, (kind, c0, ncol) in enumerate(groups):
                sl = gi % 2
                wsl = ws[sl][:, :, 0:ncol]
                self.ld(wsl, W[l, :, c0:c0 + ncol].rearrange("(kc p) n -> p kc n", p=128), [], [f"ws{sl}"], dkey=f"ws{sl}", eng="pool")
                if kind == "g":
                    gsel, chunk0 = divmod((c0 - C_GA) // 128, KC)
                    for cc in range(4):
                        for tb in range(T // 512):
                            bank = cnt["ps"] % 4; cnt["ps"] += 1
                            ps = self.psum[bank]
                            for kc in range(KC):
                                self.mm(ps, wsl[:, kc, cc * 128:(cc + 1) * 128], hT[:, kc, tb * 512:(tb + 1) * 512], kc == 0, kc == KC - 1,
                                        [f"ws{sl}", "hT"], [f"ps{bank}"])
                            gs = cnt["gst"] % 2; cnt["gst"] += 1
                            self.act(gst[gs], ps, AF.Sigmoid, [f"ps{bank}"], [f"gst{gs}"])
                            self.ld(self.GT_d[gsel, chunk0 + cc, :, tb * 512:(tb + 1) * 512], gst[gs], [f"gst{gs}"], ["GT_d"], dkey=f"gst{gs}")
                    continue
                for tt in range(NT):
                    gt = ps_ * NT + tt
                    tab = (self.own0 if ps_ == 0 else self.oth0) // 128 + tt
                    bank = cnt["ps"] % 4; cnt["ps"] += 1
                    ps = self.psum[bank][:, 0:ncol]
                    for kc in range(KC):
                        self.mm(ps, hT[:, kc, tt * 128:(tt + 1) * 128], wsl[:, kc, :], kc == 0, kc == KC - 1, [f"ws{sl}", "hT"], [f"ps{bank}"])
                    pk = [f"ps{bank}"]
                    if kind in ("dq", "dk"):
                        r = cnt["rb"] % 2; cnt["rb"] += 1
                        self.act(rb[r], ps, AF.Copy, pk, [f"rb{r}"])
                        ps3 = ps.rearrange("p (s d) -> p s d", d=64)
                        rb3 = rb[r].rearrange("p (s d) -> p s d", d=64)
                        self.rope((t1, t2), ps3, rb3, self.CCp[:, tab, :], self.SSp[:, tab, :], 16, 8, pk + ([f"rb{r}"] if os.environ.get("KSER") else []), [f"rb{r}"], "rp")
                        tbank = TB0 + cnt["qts"] % 2
                        q = cnt["qts"] % 2; cnt["qts"] += 1
                        pt = self.psum[tbank].bitcast(BF16)[:, 0:512].rearrange("p (a b) -> p a b", b=128)
                        for j in range(4):
                            self.tr(pt[:, j, :], rb[r][:, j * 128:(j + 1) * 128], [f"rb{r}"], [f"ps{tbank}"])
                        q3 = qts[q].rearrange("p (a b) -> p a b", b=128)
                        self.op("dve", "tensor_copy", [f"ps{tbank}"], [f"qts{q}"], out=q3, in_=pt)
                        h0 = ((c0 - (C_DQ if kind == "dq" else C_DK)) // 128)
                        if kind == "dq":
                            dst = self.QT_d[h0:h0 + 4, :, tt * 128:(tt + 1) * 128].rearrange("h p t -> p h t")
                            self.ld(dst, q3, [f"qts{q}"], ["QT_d"], dkey=f"qts{q}")
                        else:
                            dst = self.KT_d[h0:h0 + 4, :, gt * 128:(gt + 1) * 128].rearrange("h p t -> p h t")
                            self.ld(dst, q3, [f"qts{q}"], ["KT_d"], dkey=f"qts{q}")
                    elif kind == "dv":
                        r = cnt["rb"] % 2; cnt["rb"] += 1
                        self.act(rb[r], ps, AF.Copy, pk, [f"rb{r}"])
                        self.ld(self.V_d[gt * 128:(gt + 1) * 128, c0 - C_DV:c0 - C_DV + 512], rb[r], [f"rb{r}"], ["V_d"], dkey=f"rb{r}")
                    elif kind == "cq":
                        self.act(cqj, ps, AF.Square, pk, ["cqj", "cqs"], accum_out=cqs)
                        self.act(cqs, cqs, AF.Sqrt, ["cqs"], ["cqs"], bias=1e-6, scale=1.0 / 512)
                        self.op("dve", "reciprocal", ["cqs"], ["cqs"], out=cqs, in_=cqs)
                        r = cnt["rb"] % 2; cnt["rb"] += 1
                        self.op("dve", "scalar_tensor_tensor", pk + ["cqs", "gq_b"], [f"rb{r}"], out=rb[r], in0=ps, scalar=cqs[:, 0:1], in1=self.gq_b,
                                op0=ALU.mult, op1=ALU.mult)
                        tbank = TB0 + cnt["qts"] % 2; cnt["qts"] += 1
                        pt = self.psum[tbank].bitcast(BF16)[:, 0:512].rearrange("p (a b) -> p a b", b=128)
                        for j in range(4):
                            self.tr(pt[:, j, :], rb[r][:, j * 128:(j + 1) * 128], [f"rb{r}"], [f"ps{tbank}"])
                        self.op("dve", "tensor_copy", [f"ps{tbank}"], ["cqT"], out=self.cqT[:, :, tt * 128:(tt + 1) * 128], in_=pt)
                    elif kind == "ckv":
                        self.act(cqj[:, 0:256], ps[:, 0:256], AF.Square, pk, ["cqj", "cqs"], accum_out=cqs)
                        self.act(cqs, cqs, AF.Sqrt, ["cqs"], ["cqs"], bias=1e-6, scale=1.0 / 256)
                        self.op("dve", "reciprocal", ["cqs"], ["cqs"], out=cqs, in_=cqs)
                        r = cnt["rb"] % 2; cnt["rb"] += 1
                        self.op("dve", "scalar_tensor_tensor", pk + ["cqs", "gkv_b"], [f"rb{r}"], out=rb[r][:, 0:256], in0=ps[:, 0:256], scalar=cqs[:, 0:1],
                                in1=self.gkv_b, op0=ALU.mult, op1=ALU.mult)
                        ps3 = ps[:, 256:320].rearrange("p (s d) -> p s d", d=64)
                        k3 = krb[:, 0:64].rearrange("p (s d) -> p s d", d=64)
                        self.rope((t1m, t2m), ps3, k3, self.CCm[:, tab, :], self.SSm[:, tab, :], 64, 1, pk, ["krb"], "rm")
                        self.op("dve", "tensor_copy", ["krb"], ["krb"], out=krb[:, 64:128], in_=krb[:, 0:64])
                        tbank = TB0 + cnt["qts"] % 2; cnt["qts"] += 1
                        pt = self.psum[tbank].bitcast(BF16)[:, 0:384].rearrange("p (a b) -> p a b", b=128)
                        for j in range(2):
                            self.tr(pt[:, j, :], rb[r][:, j * 128:(j + 1) * 128], [f"rb{r}"], [f"ps{tbank}"])
                        self.tr(pt[:, 2, :], krb, ["krb"], [f"ps{tbank}"])
                        self.op("dve", "tensor_copy", [f"ps{tbank}"], ["ckvT"], out=self.ckvT[:, :, gt * 128:(gt + 1) * 128], in_=pt[:, 0:2, :])
                        self.act(self.KrT2[:, gt * 128:(gt + 1) * 128], pt[:, 2, :], AF.Copy, [f"ps{tbank}"], ["KrT2"])
            P.barrier()
        for n, t_, k_ in (("dbg_cqT", self.cqT, "cqT"), ("dbg_ckvT", self.ckvT, "ckvT"), ("dbg_KrT2", self.KrT2, "KrT2")):
            if n in self.dbg_o:
                self.out_ops.append(self.ld(self.dbg_o[n], t_, [k_], [], dkey="dbg"))

    def phase_final(self):
        b = self.Bump(self, OFF_A)
        g_b = b([128, D], F32)
        xt = [b([128, D], F32) for _ in range(2)]
        yt = [b([128, D], F32) for _ in range(2)]
        jk = b([128, D], BF16)
        ss = [b([128, 1], F32) for _ in range(2)]
        self.ld(g_b, self.w["g_final"].partition_broadcast(128), [], ["fg_b"], dkey="n_g")
        for tt in range(NT):
            g = tt % 2
            rs = slice(tt * 128, (tt + 1) * 128)
            self.ld(xt[g], self.xres[rs, :], ["xres"], [f"fxt{g}"], dkey=f"xo{g}")
            self.act(jk, xt[g], AF.Square, [f"fxt{g}"], ["fjk", f"fss{g}"], accum_out=ss[g])
            self.act(ss[g], ss[g], AF.Sqrt, [f"fss{g}"], [f"fss{g}"], bias=1e-6, scale=1.0 / D)
            self.op("dve", "reciprocal", [f"fss{g}"], [f"fss{g}"], out=ss[g], in_=ss[g])
            self.op("dve", "scalar_tensor_tensor", [f"fxt{g}", f"fss{g}", "fg_b"], [f"fyt{g}"], out=yt[g], in0=xt[g], scalar=ss[g][:, 0:1], in1=g_b,
                    op0=ALU.mult, op1=ALU.mult)
            self.out_ops.append(self.ld(self.out[rs, :], yt[g], [f"fyt{g}"], [], dkey=f"xn{g}"))

    def phase_copy_out(self):
        for i in range(4):
            rs = slice(i * 512, (i + 1) * 512)
            self.out_ops.append(self.ld(self.out[rs, :], self.xres[rs, :], ["xres"], [], dkey=f"co{i}"))


_WNAMES = ["g_mix", "w_in", "lambda_q1", "lambda_k1", "lambda_q2", "lambda_k2", "g_subln", "g_q_latent", "w_q_up", "g_kv_latent",
           "w_kv_up", "w_branch_diff", "w_branch_mla", "w_out", "g_ffn", "w_gate_up", "w_down", "w_ple_in", "g_ple", "w_ple_gate", "g_final"]


def core_inputs(c, x_full, p, positions, weights):
    b, hf = divmod(c, 2)
    own = slice(hf * T, (hf + 1) * T)
    oth = slice((1 - hf) * T, (2 - hf) * T)
    xin = np.ascontiguousarray(np.concatenate([x_full[b, own], x_full[b, oth]], axis=0))
    pos = np.concatenate([positions[b, own], positions[b, oth]], axis=0).astype(np.int32)
    pos = np.ascontiguousarray(pos.reshape(NS, 128).T)
    pin = np.ascontiguousarray(np.concatenate([p[:, b, own, :], p[:, b, oth, :]], axis=1))
    d = {"xin": xin, "pos": pos, "pin": pin}
    d.update(weights)
    return d


def kernel(**inputs):
    x = np.asarray(inputs["x"], dtype=np.float32)
    p = np.asarray(inputs["p"], dtype=np.float32)
    positions = np.asarray(inputs["positions"])
    weights = {n: np.ascontiguousarray(np.asarray(inputs[n], dtype=np.float32)) for n in _WNAMES}
    nc = Builder([(0, "xin", 0, T), (0, "xin", T, 0), (1, "xres", 0, T)], final_norm=True).build()
    in_maps = [core_inputs(c, x, p, positions, weights) for c in range(8)]
    res = run_bass_kernel_spmd(nc, in_maps, core_ids=list(range(8)))
    out = np.empty_like(x)
    for c in range(8):
        b, hf = divmod(c, 2)
        out[b, hf * T:(hf + 1) * T] = res.results[c]["out"]
    return out
```

```python
import math
import os
from contextlib import ExitStack

import numpy as np
import concourse.bass as bass
import concourse.mybir as mybir
from concourse.bass_utils import run_bass_kernel_spmd

F32 = mybir.dt.float32
BF16 = mybir.dt.bfloat16
I32 = mybir.dt.int32
U8 = mybir.dt.uint8
AF = mybir.ActivationFunctionType
ALU = mybir.AluOpType
PI = float(np.pi)

D = 2048
T = 2048
S = 4096
NT = T // 128
NS = S // 128
KC = D // 128
INC = 8000
DFF = 5632
FC = DFF // 128
DEPTH = 2
C_DQ, C_DK, C_DV, C_CQ, C_CKV, C_KR, C_GA, C_GB = 0, 1024, 2048, 3072, 3584, 3840, 3904, 5952
ROPE_THETA = 500000.0
TB0 = int(os.environ.get("KTB0", "6"))


class Op:
    __slots__ = ("eng", "fn", "deps", "is_dma", "dkey", "idx", "sem", "val", "signal", "name")

    def __init__(self, eng, fn, is_dma=False, dkey=None, name=""):
        self.eng = eng
        self.fn = fn
        self.deps = []
        self.is_dma = is_dma
        self.dkey = dkey
        self.sem = None
        self.val = None
        self.signal = False
        self.name = name


class Plan:
    ENGS = ("pe", "act", "dve", "pool", "sp")

    def __init__(self):
        self.ops = []
        self.last_w = {}
        self.readers = {}
        self.last_eng = {}
        self.last_dma = {}
        self.bar = {}

    def add(self, eng, fn, reads=(), writes=(), is_dma=False, dkey=None, name="", extra_deps=(), exclude=()):
        op = Op(eng, fn, is_dma, dkey, name)
        op.idx = len(self.ops)
        deps = set()
        extra = list(extra_deps)
        if eng in self.bar:
            extra += self.bar.pop(eng)
        xr = []
        for b in reads:
            w = self.last_w.get(b)
            if w is not None:
                deps.add(w)
            if isinstance(b, str) and b.startswith("ps"):
                for r_ in self.readers.get(b, ()):
                    if r_.eng != eng:
                        xr.append(r_)
        extra += xr
        for b in writes:
            w = self.last_w.get(b)
            if w is not None:
                deps.add(w)
            for r in self.readers.get(b, ()):
                deps.add(r)
        extra = [d for d in extra if d is not None]
        for d in extra:
            deps.add(d)
        pruned = []
        for d in deps:
            if d in exclude:
                continue
            if (not d.is_dma) and (not op.is_dma) and d.eng == op.eng and d.eng == "pe":
                continue
            pruned.append(d)
        op.deps = sorted(pruned, key=lambda o: o.idx)
        for d in op.deps:
            d.signal = True
        for b in reads:
            self.readers.setdefault(b, []).append(op)
        for b in writes:
            self.last_w[b] = op
            self.readers[b] = []
        self.ops.append(op)
        if is_dma:
            self.last_dma[dkey] = op
        else:
            self.last_eng[eng] = op
        return op

    def dma(self, eng, fn, reads=(), writes=(), dkey=None, name="", extra_deps=(), exclude=()):
        assert dkey is not None
        return self.add(eng, fn, reads, writes, is_dma=True, dkey=dkey, name=name, extra_deps=extra_deps, exclude=exclude)

    def barrier(self):
        if os.environ.get("KMARK"):
            print("barrier at op", len(self.ops))
        deps = list(self.last_eng.values()) + list(self.last_dma.values())
        self.bar = {e: list(deps) for e in self.ENGS}

    def emit(self, nc, es, final_wait_ops=()):
        sems = {}

        def get_sem(key):
            if key not in sems:
                nm = "s_" + "_".join(str(k) for k in (key if isinstance(key, tuple) else (key,)))
                sems[key] = es.enter_context(nc.semaphore(nm))
            return sems[key]

        maxops = int(os.environ.get("KMAXOPS", "0"))
        if maxops:
            self.ops = self.ops[:maxops]
            final_wait_ops = [o for o in final_wait_ops if o.idx < maxops]
            for o in self.ops:
                o.signal = False
            for o in self.ops:
                for d_ in o.deps:
                    d_.signal = True
        counters = {}
        for op in self.ops:
            if op.is_dma:
                key = ("d", op.dkey)
                op.signal = True
                counters[key] = counters.get(key, 0) + 16
            else:
                if not op.signal:
                    continue
                key = ("e", op.eng)
                counters[key] = counters.get(key, 0) + 1
            op.sem = get_sem(key)
            op.val = counters[key]
        self.n_sems = len(sems)
        per_eng = {e: [] for e in self.ENGS}
        for op in self.ops:
            per_eng[op.eng].append(op)
        final_wait_ops = list(final_wait_ops)
        block = es.enter_context(nc.Block())

        def run(eng_name, eng):
            known = {}

            def waits(dl):
                need = {}
                for d in dl:
                    k = id(d.sem)
                    if known.get(k, 0) >= d.val:
                        continue
                    if k not in need or need[k][1] < d.val:
                        need[k] = (d.sem, d.val)
                for k, (s, v) in need.items():
                    eng.wait_ge(s, v)
                    known[k] = v

            for op in per_eng[eng_name]:
                waits(op.deps)
                ins = op.fn(eng)
                if op.signal:
                    ins.then_inc(op.sem, 16 if op.is_dma else 1)
            if eng_name == "sp":
                waits(final_wait_ops)

        @block.tensor
        def _(e):
            run("pe", e)

        @block.scalar
        def _(e):
            run("act", e)

        @block.vector
        def _(e):
            run("dve", e)

        @block.gpsimd
        def _(e):
            run("pool", e)

        @block.sync
        def _(e):
            run("sp", e)


ARENA_BYTES = 212000
OFF_A = 25 * 1024
OFF_B = 65 * 1024
OFF_C = 129 * 1024


class Builder:
    def __init__(self, layers, final_norm, dbg=()):
        self.layers = list(layers)
        self.final_norm = final_norm
        self.dbg = set(dbg)
        self.nc = bass.Bass("TRN2", target_bir_lowering=False)
        self.P = Plan()
        self.out_ops = []

    def dram(self, name, shape, dt, kind="Internal"):
        if kind == "Internal" and name in self.dbg:
            kind = "ExternalOutput"
        return self.nc.dram_tensor(name, list(shape), dt, kind=kind).ap()

    def carve(self, off, shape, dt):
        n = int(np.prod(shape[1:])) * mybir.dt.size(dt)
        off = (off + 31) // 32 * 32
        assert off + n <= ARENA_BYTES, (off, n)
        ap = self.arena[0:shape[0], off:off + n].bitcast(dt)
        if len(shape) == 3:
            ap = ap.rearrange("p (a b) -> p a b", b=shape[2])
        elif len(shape) == 4:
            ap = ap.rearrange("p (a b c) -> p a b c", b=shape[2], c=shape[3])
        return ap, off + n

    class Bump:
        def __init__(self, b, off, limit=ARENA_BYTES):
            self.b, self.off, self.limit = b, off, limit

        def __call__(self, shape, dt):
            ap, self.off = self.b.carve(self.off, shape, dt)
            assert self.off <= self.limit, (self.off, self.limit)
            return ap

    def mm(self, out, lhsT, rhs, start, stop, r, w):
        return self.P.add("pe", lambda e: e.matmul(out, lhsT=lhsT, rhs=rhs, start=start, stop=stop), r, w)

    def tr(self, out, in_, r, w):
        ident = self.ident[0:in_.shape[0], 0:in_.shape[0]]
        return self.P.add("pe", lambda e: e.transpose(out, in_, ident), r, w)

    def act(self, out, in_, func, r, w, **kw):
        return self.P.add("act", lambda e: e.activation(out=out, in_=in_, func=func, **kw), r, w)

    def op(self, eng, name, r, w, **kw):
        return self.P.add(eng, lambda e: getattr(e, name)(**kw), r, w)

    def ld(self, out, in_, r, w, dkey, eng="sp"):
        if len(out.shape) == 3 and out.shape[1] > 4 and len(in_.shape) == 3:
            pieces = []
            for i in range(0, out.shape[1], 4):
                o_, i_ = out[:, i:i + 4, :], in_[:, i:i + 4, :]
                pieces.append(self.P.dma(eng, (lambda o_, i_: (lambda e: e.dma_start(out=o_, in_=i_)))(o_, i_), r, w, dkey=dkey,
                                         exclude=tuple(pieces)))
            return pieces[-1]
        return self.P.dma(eng, lambda e: e.dma_start(out=out, in_=in_), r, w, dkey=dkey)

    def build(self):
        nc = self.nc
        L = len(self.layers)
        ein = lambda n, sh, dt=F32: nc.dram_tensor(n, list(sh), dt, kind="ExternalInput").ap()
        self.xin = ein("xin", [S, D])
        self.pos = ein("pos", [128, NS], I32)
        self.pin = ein("pin", [DEPTH, S, 256])
        self.w = {}
        for n, sh in [("g_mix", [DEPTH, D]), ("w_in", [DEPTH, D, INC]), ("lambda_q1", [DEPTH, 64]), ("lambda_k1", [DEPTH, 64]),
                      ("lambda_q2", [DEPTH, 64]), ("lambda_k2", [DEPTH, 64]), ("g_subln", [DEPTH, 128]), ("g_q_latent", [DEPTH, 512]),
                      ("w_q_up", [DEPTH, 512, 1536]), ("g_kv_latent", [DEPTH, 256]), ("w_kv_up", [DEPTH, 256, 2048]),
                      ("w_branch_diff", [DEPTH, 1024, D]), ("w_branch_mla", [DEPTH, 1024, D]), ("w_out", [DEPTH, D, D]),
                      ("g_ffn", [DEPTH, D]), ("w_gate_up", [DEPTH, D, 2 * DFF]), ("w_down", [DEPTH, DFF, D]),
                      ("w_ple_in", [DEPTH, 256, D]), ("g_ple", [DEPTH, D]), ("w_ple_gate", [DEPTH, D, D]), ("g_final", [D])]:
            self.w[n] = ein(n, sh)
        self.out = nc.dram_tensor("out", [T, D], F32, kind="ExternalOutput").ap()
        self.xres = self.dram("xres", [S, D], F32)
        self.QT_d = self.dram("QT_d", [8, 128, T], BF16)
        self.KT_d = self.dram("KT_d", [8, 128, S], BF16)
        self.V_d = self.dram("V_d", [S, 1024], BF16)
        self.QnT_d = self.dram("QnT_d", [8, 128, T], BF16)
        self.QrT_d = self.dram("QrT_d", [4, 128, T], BF16)
        self.KnT_d = self.dram("KnT_d", [8, 128, S], BF16)
        self.Vm_d = self.dram("Vm_d", [S, 1024], BF16)
        self.GT_d = self.dram("GT_d", [2, KC, 128, T], F32)
        self.actT_d = self.dram("actT_d", [NT, 128, FC, 128], BF16)
        self.KrT2_d = self.dram("KrT2_d", [128, S], BF16)
        self.dbg_o = {}
        for n, sh, dt in [("dbg_hT", [128, KC, T], BF16), ("dbg_cqT", [128, 4, T], BF16), ("dbg_ckvT", [128, 2, S], BF16),
                          ("dbg_KrT2", [128, S], BF16), ("dbg_odT", [128, 8, T], BF16), ("dbg_omT", [128, 8, T], BF16),
                          ("dbg_mT", [128, KC, T], BF16), ("dbg_tab", [128, NS, 64], F32)]:
            if n in self.dbg:
                self.dbg_o[n] = nc.dram_tensor(n, sh, dt, kind="ExternalOutput").ap()

        with ExitStack() as es:
            self.arena = es.enter_context(nc.sbuf_tensor("arena", [128, ARENA_BYTES], U8))
            self.psum = [es.enter_context(nc.psum_tensor(f"ps{i}", [128, 512], F32))[:] for i in range(8)]
            pb = self.Bump(self, 0, OFF_A)
            self.ident = pb([128, 128], BF16)
            self.ones_bf = pb([128, 128], BF16)
            self.ones_f = pb([128, 128], F32)
            self.CCp = pb([128, NS, 16], F32)
            self.SSp = pb([128, NS, 16], F32)
            self.CCm = pb([128, NS, 64], F32)
            self.SSm = pb([128, NS, 64], F32)
            self.gq_b = pb([128, 512], F32)
            self.gkv_b = pb([128, 256], F32)
            self.gs_sub = pb([128, 1], F32)
            self.lamt = pb([128, 8], F32)
            self.persist_end = pb.off

            self.phase_consts()
            for (layer, src_name, own0, oth0, *rest) in self.layers:
                self.own0, self.oth0 = own0, oth0
                self.skip_kv = bool(rest and rest[0])
                self.src = self.xin if src_name == "xin" else self.xres
                self.layer(layer)
            if self.final_norm:
                self.phase_final()
            else:
                self.phase_copy_out()
            self.P.emit(nc, es, final_wait_ops=self.out_ops)
        return nc

    def phase_consts(self):
        P = self.P
        b = self.Bump(self, OFF_C)
        identf = b([128, 128], F32)
        self.op("pool", "memset", [], ["identf"], ap=identf, constant=0.0)
        P.add("pool", lambda e: e.affine_select(out=identf, in_=identf, compare_op=ALU.not_equal, fill=1.0, base=0,
                                                 pattern=[[-1, 128]], channel_multiplier=1), ["identf"], ["identf"])
        self.op("pool", "tensor_copy", ["identf"], ["ident"], out=self.ident, in_=identf)
        self.op("pool", "memset", [], ["ones_bf"], ap=self.ones_bf, constant=1.0)
        self.op("pool", "memset", [], ["ones_f"], ap=self.ones_f, constant=1.0 / 128.0)
        post = b([128, NS], I32)
        posf = b([128, NS], F32)
        self.ld(post, self.pos, [], ["post"], dkey="c_pos")
        self.op("dve", "tensor_copy", ["post"], ["posf"], out=posf, in_=post)
        for nm, nf, CC, SS in (("p", 8, self.CCp, self.SSp), ("m", 32, self.CCm, self.SSm)):
            dim = 2 * nf
            invf = b([128, nf], F32)
            for j in range(nf):
                self.op("dve", "memset", [], ["invf" + nm], ap=invf[:, j:j + 1], constant=float(np.float32(ROPE_THETA) ** np.float32(-(2 * j) / dim)))
            sh = [128, NS, nf]
            ang = b(sh, F32); uu = b(sh, F32); ki = b(sh, I32); kf = b(sh, F32); r1 = b(sh, F32); r2 = b(sh, F32)
            mm_ = b(sh, F32); rs = b(sh, F32); rc = b(sh, F32); sn = b(sh, F32); cs = b(sh, F32)
            k = lambda s_: s_ + nm
            self.op("dve", "tensor_tensor", ["posf", k("invf")], [k("ang")], out=ang, in0=posf.unsqueeze(2).broadcast_to(sh),
                    in1=invf.unsqueeze(1).broadcast_to(sh), op=ALU.mult)
            self.op("dve", "tensor_scalar", [k("ang")], [k("uu")], out=uu, in0=ang, scalar1=float(1 / (2 * np.pi)), scalar2=None, op0=ALU.mult)
            self.op("dve", "tensor_copy", [k("uu")], [k("ki")], out=ki, in_=uu)
            self.op("dve", "tensor_copy", [k("ki")], [k("kf")], out=kf, in_=ki)
            C1 = 6.28125
            C2 = float(2 * np.pi - 6.28125)
            self.op("dve", "scalar_tensor_tensor", [k("kf"), k("ang")], [k("r1")], out=r1, in0=kf, scalar=-C1, in1=ang, op0=ALU.mult, op1=ALU.add)
            self.op("dve", "scalar_tensor_tensor", [k("kf"), k("r1")], [k("r2")], out=r2, in0=kf, scalar=-C2, in1=r1, op0=ALU.mult, op1=ALU.add)

            def wrap(dst, src, shift, kd, ks):
                self.op("dve", "tensor_scalar", [ks], [k("mm")], out=mm_, in0=src, scalar1=float(PI - shift), scalar2=float(-2 * PI), op0=ALU.is_ge, op1=ALU.mult)
                self.op("dve", "scalar_tensor_tensor", [ks, k("mm")], [kd], out=dst, in0=src, scalar=float(shift), in1=mm_, op0=ALU.add, op1=ALU.add)
                self.op("dve", "tensor_scalar", [kd], [kd], out=dst, in0=dst, scalar1=-PI, scalar2=PI, op0=ALU.max, op1=ALU.min)

            wrap(rs, r2, 0.0, k("rs"), k("r2"))
            wrap(rc, rs, PI / 2, k("rc"), k("rs"))
            self.act(sn, rs, AF.Sin, [k("rs")], [k("sn")])
            self.act(cs, rc, AF.Sin, [k("rc")], [k("cs")])
            tk = "tab" + nm
            self.op("dve", "tensor_copy", [k("cs")], [tk + "c0"], out=CC[:, :, 0:nf], in_=cs)
            self.op("dve", "tensor_copy", [k("cs")], [tk + "c1"], out=CC[:, :, nf:dim], in_=cs)
            self.op("dve", "tensor_scalar", [k("sn")], [tk + "s0"], out=SS[:, :, 0:nf], in0=sn, scalar1=-1.0, scalar2=None, op0=ALU.mult)
            self.op("dve", "tensor_copy", [k("sn")], [tk + "s1"], out=SS[:, :, nf:dim], in_=sn)
        if "dbg_tab" in self.dbg_o:
            self.out_ops.append(self.ld(self.dbg_o["dbg_tab"], self.CCm, ["tabmc0", "tabmc1"], [], dkey="dbg"))
        P.barrier()

    def rope(self, b_tmp, ps3, out3, CC, SS, rd, n, rk, wk, tag):
        t1, t2 = b_tmp
        hf = rd // 2
        shp = [128, n, rd]
        bc = lambda a: a.unsqueeze(1).broadcast_to([128, n, a.shape[-1]])
        self.op("dve", "tensor_tensor", rk, [tag + "t1"], out=t1, in0=ps3[:, :, 0:rd], in1=bc(CC), op=ALU.mult)
        self.op("dve", "tensor_tensor", rk, [tag + "t2a"], out=t2[:, :, 0:hf], in0=ps3[:, :, hf:rd], in1=bc(SS[:, 0:hf]), op=ALU.mult)
        self.op("dve", "tensor_tensor", rk, [tag + "t2b"], out=t2[:, :, hf:rd], in0=ps3[:, :, 0:hf], in1=bc(SS[:, hf:rd]), op=ALU.mult)
        return self.op("dve", "tensor_tensor", [tag + "t1", tag + "t2a", tag + "t2b"], wk, out=out3[:, :, 0:rd], in0=t1, in1=t2, op=ALU.add)

    def norm_pass(self, b, x_src, row0, ntiles, g_ap, hT, hT_key, tag):
        g_b = b([128, D], F32)
        xt = b([128, D], F32)
        hb = [b([128, D], BF16) for _ in range(2)]
        ss = [b([128, 1], F32) for _ in range(2)]
        self.ld(g_b, g_ap.partition_broadcast(128), [], [tag + "g_b"], dkey="n_g")
        for tt in range(ntiles):
            sl = tt % 2
            self.ld(xt, x_src[row0 + tt * 128: row0 + (tt + 1) * 128, :], [x_src.tensor.name], [tag + "xt"], dkey="n_xt")
            self.act(hb[sl], xt, AF.Square, [tag + "xt"], [tag + f"hb{sl}", tag + f"ss{sl}"], accum_out=ss[sl])
            self.act(ss[sl], ss[sl], AF.Sqrt, [tag + f"ss{sl}"], [tag + f"ss{sl}"], bias=1e-6, scale=1.0 / D)
            self.op("dve", "reciprocal", [tag + f"ss{sl}"], [tag + f"ss{sl}"], out=ss[sl], in_=ss[sl])
            self.op("dve", "scalar_tensor_tensor", [tag + "xt", tag + f"ss{sl}", tag + "g_b"], [tag + f"hb{sl}"], out=hb[sl], in0=xt,
                    scalar=ss[sl][:, 0:1], in1=g_b, op0=ALU.mult, op1=ALU.mult)
            for half in range(2):
                bank = 4 + (tt * 2 + half) % 2
                pt = self.psum[bank].bitcast(BF16).rearrange("p (a b) -> p a b", b=128)
                for j in range(8):
                    kc = half * 8 + j
                    self.tr(pt[:, j, :], hb[sl][:, kc * 128:(kc + 1) * 128], [tag + f"hb{sl}"], [f"ps{bank}"])
                dst = hT[:, half * 8:(half + 1) * 8, tt * 128:(tt + 1) * 128]
                if half == 0:
                    self.act(dst, pt, AF.Copy, [f"ps{bank}"], [hT_key])
                else:
                    self.op("dve", "tensor_copy", [f"ps{bank}"], [hT_key], out=dst, in_=pt)

    def layer(self, l):
        stop = os.environ.get("KSTOP", "")
        self.phase_z(l)
        if stop == "z":
            return
        self.phase_up(l)
        if stop == "up":
            return
        self.phase_attn(l)
        if stop == "attn":
            return
        self.phase_branch(l)
        self.phase_wout(l)
        if stop == "wout":
            return
        self.phase_ffn(l)
        if stop == "ffn":
            return
        self.phase_ple(l)

    def evac(self, i, out, in_, r, w):
        if i % 2 == 0:
            return self.act(out, in_, AF.Copy, r, w)
        return self.op("dve", "tensor_copy", r, w, out=out, in_=in_)

    def phase_up(self, l):
        P = self.P
        b = self.Bump(self, OFF_C)
        wq = b([128, 4, 1536], BF16)
        wkv = b([128, 2, 2048], BF16)
        self.ld(wq, self.w["w_q_up"][l].rearrange("(kc p) n -> p kc n", p=128), [], ["wq"], dkey="ws0", eng="pool")
        if not self.skip_kv:
            self.ld(wkv, self.w["w_kv_up"][l].rearrange("(kc p) n -> p kc n", p=128), [], ["wkv"], dkey="ws1", eng="pool")
        st = [b([128, 512], BF16) for _ in range(2)]
        rbm = [b([128, 512], BF16) for _ in range(2)]
        t1 = b([128, 8, 64], F32); t2 = b([128, 8, 64], F32)
        n = {"ps": 0, "st": 0, "rb": 0, "tp": 0}

        def nxt(k, m):
            v = n[k] % m; n[k] += 1
            return v
        wq3 = wq.rearrange("p k (h c) -> p k h c", c=192)
        wkv3 = wkv.rearrange("p k (h c) -> p k h c", c=256)
        for h in range(8):
            for tb in range(T // 512):
                bank = nxt("ps", 4); ps = self.psum[bank]
                for kc in range(4):
                    self.mm(ps, wq3[:, kc, h, 0:128], self.cqT[:, kc, tb * 512:(tb + 1) * 512], kc == 0, kc == 3, ["wq", "cqT"], [f"ps{bank}"])
                sl = nxt("st", 2)
                self.evac(sl, st[sl], ps, [f"ps{bank}"], [f"st{sl}"])
                self.ld(self.QnT_d[h, :, tb * 512:(tb + 1) * 512], st[sl], [f"st{sl}"], ["QnT_d"], dkey=f"st{sl}")
        for tt in range(NT):
            bank = nxt("ps", 4); ps = self.psum[bank]
            ps3 = ps.rearrange("p (h r) -> p h r", r=64)
            for kc in range(4):
                self.mm(ps3, self.cqT[:, kc, tt * 128:(tt + 1) * 128], wq3[:, kc, :, 128:192], kc == 0, kc == 3, ["wq", "cqT"], [f"ps{bank}"])
            r = nxt("rb", 2)
            rb3 = rbm[r].rearrange("p (h r) -> p h r", r=64)
            self.rope((t1, t2), ps3, rb3, self.CCm[:, self.own0 // 128 + tt, :], self.SSm[:, self.own0 // 128 + tt, :], 64, 8, [f"ps{bank}"], [f"rbm{r}"], "rq")
            tbank = 6 + nxt("tp", 2)
            pt = self.psum[tbank].bitcast(BF16)[:, 0:512].rearrange("p (a b) -> p a b", b=128)
            for j in range(4):
                self.tr(pt[:, j, :], rbm[r][:, j * 128:(j + 1) * 128], [f"rbm{r}"], [f"ps{tbank}"])
            sl = nxt("st", 2)
            s3 = st[sl].rearrange("p (a b) -> p a b", b=128)
            self.evac(sl, s3, pt, [f"ps{tbank}"], [f"st{sl}"])
            self.ld(self.QrT_d[:, :, tt * 128:(tt + 1) * 128].rearrange("h p t -> p h t"), s3, [f"st{sl}"], ["QrT_d"], dkey=f"st{sl}")
        for h in range(0 if self.skip_kv else 8):
            for tb in range(S // 512):
                bank = nxt("ps", 4); ps = self.psum[bank]
                for kc in range(2):
                    self.mm(ps, wkv3[:, kc, h, 0:128], self.ckvT[:, kc, tb * 512:(tb + 1) * 512], kc == 0, kc == 1, ["wkv", "ckvT"], [f"ps{bank}"])
                sl = nxt("st", 2)
                self.evac(sl, st[sl], ps, [f"ps{bank}"], [f"st{sl}"])
                self.ld(self.KnT_d[h, :, tb * 512:(tb + 1) * 512], st[sl], [f"st{sl}"], ["KnT_d"], dkey=f"st{sl}")
        for gt in range(0 if self.skip_kv else NS):
            for half in range(2):
                bank = nxt("ps", 4); ps = self.psum[bank]
                ps3 = ps.rearrange("p (h r) -> p h r", r=128)
                for kc in range(2):
                    self.mm(ps3, self.ckvT[:, kc, gt * 128:(gt + 1) * 128], wkv3[:, kc, half * 4:(half + 1) * 4, 128:256], kc == 0, kc == 1,
                            ["wkv", "ckvT"], [f"ps{bank}"])
                sl = nxt("st", 2)
                self.evac(sl, st[sl], ps, [f"ps{bank}"], [f"st{sl}"])
                self.ld(self.Vm_d[gt * 128:(gt + 1) * 128, half * 512:(half + 1) * 512], st[sl], [f"st{sl}"], ["Vm_d"], dkey=f"st{sl}")
        P.barrier()

    def phase_attn(self, l):
        P = self.P
        lam_init = 0.8 - 0.6 * math.exp(-0.3 * l)
        self.odT, _ = self.carve(OFF_B, [128, 8, T], BF16)
        self.omT, _ = self.carve(OFF_B + 32 * 1024, [128, 8, T], BF16)
        a = self.Bump(self, OFF_A, OFF_A + 32 * 1024)
        lq = [a([128, 64], F32) for _ in range(4)]
        lt = a([128, 64], F32)
        ls = [a([128, 1], F32) for _ in range(2)]
        nlam = a([128, 1], F32)
        gsub = a([128, 1], F32)
        r1 = a([128, 512], F32); r2 = a([128, 512], F32); Av = a([128, 512], F32); Bv = a([128, 512], F32)
        sq = a([128, 512], F32); rstd = a([128, 512], F32)
        for i, nme in enumerate(("lambda_q1", "lambda_k1", "lambda_q2", "lambda_k2")):
            self.ld(lq[i], self.w[nme][l].partition_broadcast(128), [], [f"lq{i}"], dkey=f"c_l{i}")
        self.ld(gsub, self.w["g_subln"][l].rearrange("(p o) -> p o", o=1), [], ["gsub"], dkey="c_gq")
        for j in range(2):
            self.op("dve", "tensor_tensor", [f"lq{2 * j}", f"lq{2 * j + 1}"], ["lt"], out=lt, in0=lq[2 * j], in1=lq[2 * j + 1], op=ALU.mult)
            self.op("dve", "reduce_sum", ["lt"], [f"ls{j}"], out=ls[j], in_=lt, axis=mybir.AxisListType.X)
            self.act(ls[j], ls[j], AF.Exp, [f"ls{j}"], [f"ls{j}"])
        self.op("dve", "tensor_tensor", ["ls0", "ls1"], ["nlam"], out=nlam, in0=ls[1], in1=ls[0], op=ALU.subtract)
        self.op("dve", "tensor_scalar", ["nlam"], ["nlam"], out=nlam, in0=nlam, scalar1=float(-lam_init), scalar2=None, op0=ALU.add)
        self.op("dve", "tensor_scalar", ["gsub"], ["gsub"], out=gsub, in0=gsub, scalar1=float(1.0 - lam_init), scalar2=None, op0=ALU.mult)

        b = self.Bump(self, OFF_C)
        Qb = [b([128, T], BF16) for _ in range(2)]
        Q2b = [b([128, T], BF16) for _ in range(2)]
        Kb = [b([128, S], BF16) for _ in range(2)]
        Vb = [b([128, NS, 128], BF16) for _ in range(2)]
        Pt = [b([128, 512], BF16) for _ in range(6)]
        n = {"s": 0, "p": 0}

        def nxt(k, m):
            v = n[k] % m; n[k] += 1
            return v
        sc_d = 64 ** -0.5
        sc_m = 192 ** -0.5
        for h in range(8):
            hb = h % 2
            self.ld(Qb[hb], self.QT_d[h], ["QT_d"], [f"Q{hb}"], dkey=f"aq{hb}")
            self.ld(Kb[hb], self.KT_d[h], ["KT_d"], [f"K{hb}"], dkey=f"ak{hb}")
            self.ld(Vb[hb], self.V_d[:, h * 128:(h + 1) * 128].rearrange("(kt p) e -> p kt e", p=128), ["V_d"], [f"V{hb}"], dkey=f"av{hb}")
            for qb in range(T // 512):
                qs = slice(qb * 512, (qb + 1) * 512)
                O1, O2, Z1, Z2 = self.psum[4], self.psum[5], self.psum[6], self.psum[7]
                pend = None
                for kt in range(NS + 1):
                    cur = None
                    if kt < NS:
                        ks = slice(kt * 128, (kt + 1) * 128)
                        pts = []
                        for c in range(2):
                            bank = nxt("s", 4); ps = self.psum[bank]
                            self.mm(ps, Kb[hb][c * 64:(c + 1) * 64, ks], Qb[hb][c * 64:(c + 1) * 64, qs], True, True, [f"K{hb}", f"Q{hb}"], [f"ps{bank}"])
                            p = nxt("p", 6)
                            self.act(Pt[p], ps, AF.Exp, [f"ps{bank}"], [f"Pt{p}"], scale=sc_d)
                            pts.append(p)
                        cur = (kt, pts)
                    if pend is not None:
                        kp, pts = pend
                        for c, (O, Z) in enumerate(((O1, Z1), (O2, Z2))):
                            p = pts[c]
                            self.mm(O, Vb[hb][:, kp, :], Pt[p], kp == 0, kp == NS - 1, [f"V{hb}", f"Pt{p}"], [f"ps{4 + c}"])
                            self.mm(Z, self.ones_bf, Pt[p], kp == 0, kp == NS - 1, ["ones_bf", f"Pt{p}"], [f"ps{6 + c}"])
                    pend = cur
                self.op("dve", "reciprocal", ["ps6"], ["r1"], out=r1, in_=Z1)
                self.op("dve", "reciprocal", ["ps7"], ["r2"], out=r2, in_=Z2)
                self.op("dve", "tensor_tensor", ["ps4", "r1"], ["Av"], out=Av, in0=O1, in1=r1, op=ALU.mult)
                self.op("dve", "tensor_tensor", ["ps5", "r2"], ["Bv"], out=Bv, in0=O2, in1=r2, op=ALU.mult)
                self.op("dve", "scalar_tensor_tensor", ["Bv", "Av", "nlam"], ["Av2"], out=Av, in0=Bv, scalar=nlam[:, 0:1], in1=Av, op0=ALU.mult, op1=ALU.add)
                self.op("pool", "tensor_tensor", ["Av2"], ["sq"], out=sq, in0=Av, in1=Av, op=ALU.mult)
                bank = nxt("s", 4); ms = self.psum[bank]
                self.mm(ms, self.ones_f, sq, True, True, ["ones_f", "sq"], [f"ps{bank}"])
                self.act(rstd, ms, AF.Sqrt, [f"ps{bank}"], ["rstd"], bias=1e-5, scale=1.0)
                self.op("dve", "reciprocal", ["rstd"], ["rstd"], out=rstd, in_=rstd)
                self.op("dve", "scalar_tensor_tensor", ["Av2", "gsub", "rstd"], ["odT"], out=self.odT[:, h, qs], in0=Av, scalar=gsub[:, 0:1], in1=rstd,
                        op0=ALU.mult, op1=ALU.mult)
        for h in range(8):
            hb = h % 2
            p0 = 64 * (h % 2)
            self.ld(Qb[hb], self.QnT_d[h], ["QnT_d"], [f"Q{hb}"], dkey=f"aq{hb}")
            self.ld(Q2b[hb], self.QrT_d[h // 2], ["QrT_d"], [f"Q2{hb}"], dkey=f"aq2{hb}")
            self.ld(Kb[hb], self.KnT_d[h], ["KnT_d"], [f"K{hb}"], dkey=f"ak{hb}")
            self.ld(Vb[hb], self.Vm_d[:, h * 128:(h + 1) * 128].rearrange("(kt p) e -> p kt e", p=128), ["Vm_d"], [f"V{hb}"], dkey=f"av{hb}")
            for qb in range(T // 512):
                qs = slice(qb * 512, (qb + 1) * 512)
                ob = 4 + 2 * (qb % 2)
                O, Z = self.psum[ob], self.psum[ob + 1]
                pend = None
                for kt in range(NS + 1):
                    cur = None
                    if kt < NS:
                        ks = slice(kt * 128, (kt + 1) * 128)
                        bank = nxt("s", 4); ps = self.psum[bank]
                        self.mm(ps, Kb[hb][:, ks], Qb[hb][:, qs], True, False, [f"K{hb}", f"Q{hb}"], [f"ps{bank}"])
                        self.mm(ps, self.KrT2[p0:p0 + 64, ks], Q2b[hb][p0:p0 + 64, qs], False, True, ["KrT2", f"Q2{hb}"], [f"ps{bank}"])
                        p = nxt("p", 6)
                        self.act(Pt[p], ps, AF.Exp, [f"ps{bank}"], [f"Pt{p}"], scale=sc_m)
                        cur = (kt, p)
                    if pend is not None:
                        kp, p = pend
                        self.mm(O, Vb[hb][:, kp, :], Pt[p], kp == 0, kp == NS - 1, [f"V{hb}", f"Pt{p}"], [f"ps{ob}"])
                        self.mm(Z, self.ones_bf, Pt[p], kp == 0, kp == NS - 1, ["ones_bf", f"Pt{p}"], [f"ps{ob + 1}"])
                    pend = cur
                self.op("dve", "reciprocal", [f"ps{ob + 1}"], ["r1"], out=r1, in_=Z)
                self.op("dve", "tensor_tensor", [f"ps{ob}", "r1"], ["omT"], out=self.omT[:, h, qs], in0=O, in1=r1, op=ALU.mult)
        for nme, t_, k_ in (("dbg_odT", self.odT, "odT"), ("dbg_omT", self.omT, "omT")):
            if nme in self.dbg_o:
                self.out_ops.append(self.ld(self.dbg_o[nme], t_, [k_], [], dkey="dbg"))
        P.barrier()

    def phase_branch(self, l):
        P = self.P
        self.mT, _ = self.carve(OFF_C, [128, KC, T], BF16)
        a = self.Bump(self, OFF_A, OFF_B)
        wbd = [a([128, 8, 128], BF16) for _ in range(2)]
        wbm = [a([128, 8, 128], BF16) for _ in range(2)]
        sga = [a([128, 512], F32) for _ in range(2)]
        sgb = [a([128, 512], F32) for _ in range(2)]
        m1 = [a([128, 512], F32) for _ in range(2)]
        m2 = [a([128, 512], F32) for _ in range(2)]
        it = 0
        for cc in range(KC):
            ws = cc % 2
            cs = slice(cc * 128, (cc + 1) * 128)
            self.ld(wbd[ws], self.w["w_branch_diff"][l][:, cs].rearrange("(h p) n -> p h n", p=128), [], [f"wbd{ws}"], dkey=f"ws{ws}", eng="pool")
            self.ld(wbm[ws], self.w["w_branch_mla"][l][:, cs].rearrange("(h p) n -> p h n", p=128), [], [f"wbm{ws}"], dkey=f"wsb{ws}", eng="pool")
            for tb in range(T // 512):
                ts_ = slice(tb * 512, (tb + 1) * 512)
                g = it % 2; it += 1
                bd, bm = 2 * g, 2 * g + 1
                self.ld(sga[g], self.GT_d[0, cc, :, ts_], ["GT_d"], [f"sga{g}"], dkey=f"ga{g}")
                self.ld(sgb[g], self.GT_d[1, cc, :, ts_], ["GT_d"], [f"sgb{g}"], dkey=f"gb{g}")
                for h in range(8):
                    self.mm(self.psum[bd], wbd[ws][:, h, :], self.odT[:, h, ts_], h == 0, h == 7, [f"wbd{ws}", "odT"], [f"ps{bd}"])
                for h in range(8):
                    self.mm(self.psum[bm], wbm[ws][:, h, :], self.omT[:, h, ts_], h == 0, h == 7, [f"wbm{ws}", "omT"], [f"ps{bm}"])
                self.op("dve", "tensor_tensor", [f"ps{bd}", f"sga{g}"], [f"m1{g}"], out=m1[g], in0=self.psum[bd], in1=sga[g], op=ALU.mult)
                self.op("dve", "tensor_tensor", [f"ps{bm}", f"sgb{g}"], [f"m2{g}"], out=m2[g], in0=self.psum[bm], in1=sgb[g], op=ALU.mult)
                self.op("pool", "tensor_tensor", [f"m1{g}", f"m2{g}"], ["mT"], out=self.mT[:, cc, ts_], in0=m1[g], in1=m2[g], op=ALU.add)
        if "dbg_mT" in self.dbg_o:
            self.out_ops.append(self.ld(self.dbg_o["dbg_mT"], self.mT, ["mT"], [], dkey="dbg"))
        P.barrier()

    def gemm_resid(self, lhsT_fn, nk, w_ap, w_pat, x_src, src0, tagk, pre=None):
        a = self.Bump(self, OFF_A, OFF_C)
        ws = [a([128, nk, 512], BF16) for _ in range(2)]
        xo = [a([128, 512], F32) for _ in range(2)]
        xn = [a([128, 512], F32) for _ in range(2)]
        it = 0
        for cg in range(D // 512):
            sl = cg % 2
            cs = slice(cg * 512, (cg + 1) * 512)
            self.ld(ws[sl], w_ap[:, cs].rearrange(w_pat, p=128), [], [f"ws{sl}"], dkey=f"ws{sl}", eng="pool")
            for tt in range(NT):
                rs = slice(tt * 128, (tt + 1) * 128)
                g = it % 2; it += 1
                bank = it % 4
                self.ld(xo[g], x_src[src0 + tt * 128:src0 + (tt + 1) * 128, cs], [x_src.tensor.name], [f"xo{g}"], dkey=f"xo{g}")
                for k in range(nk):
                    lt, rk = lhsT_fn(tt, k)
                    self.mm(self.psum[bank], lt, ws[sl][:, k, :], k == 0, k == nk - 1, [f"ws{sl}"] + rk, [f"ps{bank}"])
                self.op("dve", "tensor_tensor", [f"ps{bank}", f"xo{g}"], [f"xn{g}"], out=xn[g], in0=self.psum[bank], in1=xo[g], op=ALU.add)
                self.ld(self.xres[self.own0 + tt * 128:self.own0 + (tt + 1) * 128, cs], xn[g], [f"xn{g}"], ["xres"], dkey=f"xn{g}")

    def phase_wout(self, l):
        self.gemm_resid(lambda tt, k: (self.mT[:, k, tt * 128:(tt + 1) * 128], ["mT"]), KC, self.w["w_out"][l], "(kc p) n -> p kc n", self.src, self.own0, "wo")
        self.P.barrier()

    def phase_ffn(self, l):
        P = self.P
        hT, _ = self.carve(OFF_B, [128, KC, T], BF16)
        b = self.Bump(self, OFF_C)
        self.norm_pass(b, self.xres, self.own0, NT, self.w["g_ffn"][l], hT, "hT", f"f{l}")
        wg = [b([128, KC, 512], BF16) for _ in range(1)]
        wu = [b([128, KC, 512], BF16) for _ in range(1)]
        a = self.Bump(self, OFF_A, OFF_B)
        wg.append(a([128, KC, 512], BF16)); wu.append(a([128, KC, 512], BF16))
        sg = [a([128, 512], F32) for _ in range(2)]
        av = [a([128, 512], BF16) for _ in range(2)]
        Wgu = self.w["w_gate_up"][l]
        it = 0
        for fg in range(DFF // 512):
            sl = fg % 2
            self.ld(wg[sl], Wgu[:, fg * 512:(fg + 1) * 512].rearrange("(kc p) n -> p kc n", p=128), [], [f"wg{sl}"], dkey=f"ws{sl}", eng="pool")
            self.ld(wu[sl], Wgu[:, DFF + fg * 512:DFF + (fg + 1) * 512].rearrange("(kc p) n -> p kc n", p=128), [], [f"wu{sl}"], dkey=f"wsb{sl}", eng="pool")
            for j in range(4):
                fc = fg * 4 + j
                for tb in range(T // 512):
                    ts_ = slice(tb * 512, (tb + 1) * 512)
                    g = it % 2; it += 1
                    bg, bu = 2 * g, 2 * g + 1
                    for kc in range(KC):
                        self.mm(self.psum[bg], wg[sl][:, kc, j * 128:(j + 1) * 128], hT[:, kc, ts_], kc == 0, kc == KC - 1, [f"wg{sl}", "hT"], [f"ps{bg}"])
                    for kc in range(KC):
                        self.mm(self.psum[bu], wu[sl][:, kc, j * 128:(j + 1) * 128], hT[:, kc, ts_], kc == 0, kc == KC - 1, [f"wu{sl}", "hT"], [f"ps{bu}"])
                    self.act(sg[g], self.psum[bg], AF.Silu, [f"ps{bg}"], [f"sg{g}"])
                    self.op("dve", "tensor_tensor", [f"sg{g}", f"ps{bu}"], [f"av{g}"], out=av[g], in0=sg[g], in1=self.psum[bu], op=ALU.mult)
                    self.ld(self.actT_d[tb * 4:(tb + 1) * 4, :, fc, :].rearrange("j p t -> p j t"), av[g].rearrange("p (j t) -> p j t", t=128),
                            [f"av{g}"], ["actT_d"], dkey=f"av{g}")
        P.barrier()
        a = self.Bump(self, OFF_A)
        wd = [a([128, FC, 512], BF16) for _ in range(2)]
        at = [a([128, FC, 128], BF16) for _ in range(2)]
        xo = [a([128, 512], F32) for _ in range(2)]
        xn = [a([128, 512], F32) for _ in range(2)]
        it = 0
        for cg in range(D // 512):
            sl = cg % 2
            cs = slice(cg * 512, (cg + 1) * 512)
            self.ld(wd[sl], self.w["w_down"][l][:, cs].rearrange("(fc p) n -> p fc n", p=128), [], [f"wd{sl}"], dkey=f"ws{sl}", eng="pool")
            for tt in range(NT):
                rs = slice(tt * 128, (tt + 1) * 128)
                g = it % 2; it += 1
                bank = it % 4
                xs_ = slice(self.own0 + tt * 128, self.own0 + (tt + 1) * 128)
                self.ld(at[g].rearrange("p f t -> p (f t)"), self.actT_d[tt].rearrange("p f t -> p (f t)"), ["actT_d"], [f"at{g}"], dkey=f"at{g}")
                self.ld(xo[g], self.xres[xs_, cs], ["xres"], [f"xo{g}"], dkey=f"xo{g}")
                for fc in range(FC):
                    self.mm(self.psum[bank], at[g][:, fc, :], wd[sl][:, fc, :], fc == 0, fc == FC - 1, [f"wd{sl}", f"at{g}"], [f"ps{bank}"])
                self.op("dve", "tensor_tensor", [f"ps{bank}", f"xo{g}"], [f"xn{g}"], out=xn[g], in0=self.psum[bank], in1=xo[g], op=ALU.add)
                self.ld(self.xres[xs_, cs], xn[g], [f"xn{g}"], ["xres"], dkey=f"xn{g}")
        P.barrier()

    def phase_ple(self, l):
        P = self.P
        hT, _ = self.carve(OFF_B, [128, KC, T], BF16)
        b = self.Bump(self, OFF_C)
        self.norm_pass(b, self.xres, self.own0, NT, self.w["g_ple"][l], hT, "hT", f"p{l}")
        a = self.Bump(self, OFF_A, OFF_B)
        pT = a([128, 2, T], BF16)
        pf = [a([128, 256], F32) for _ in range(2)]
        pb = [a([128, 256], BF16) for _ in range(2)]
        for tt in range(NT):
            g = tt % 2
            self.ld(pf[g], self.pin[l, self.own0 + tt * 128:self.own0 + (tt + 1) * 128, :], [], [f"pf{g}"], dkey=f"xo{g}")
            self.op("pool", "tensor_copy", [f"pf{g}"], [f"pb{g}"], out=pb[g], in_=pf[g])
            tbank = 6 + tt % 2
            pt = self.psum[tbank].bitcast(BF16)[:, 0:256].rearrange("p (a b) -> p a b", b=128)
            for j in range(2):
                self.tr(pt[:, j, :], pb[g][:, j * 128:(j + 1) * 128], [f"pb{g}"], [f"ps{tbank}"])
            self.evac(tt, pT[:, :, tt * 128:(tt + 1) * 128], pt, [f"ps{tbank}"], ["pT"])
        ws = [b([128, KC, 512], BF16) for _ in range(2)]
        wi = [b([128, 2, 512], BF16) for _ in range(2)]
        sg = [a([128, 512], F32) for _ in range(2)]
        xo = [a([128, 512], F32) for _ in range(2)]
        xn = [a([128, 512], F32) for _ in range(2)]
        it = 0
        for cg in range(D // 512):
            sl = cg % 2
            cs = slice(cg * 512, (cg + 1) * 512)
            self.ld(ws[sl], self.w["w_ple_gate"][l][:, cs].rearrange("(kc p) n -> p kc n", p=128), [], [f"ws{sl}"], dkey=f"ws{sl}", eng="pool")
            self.ld(wi[sl], self.w["w_ple_in"][l][:, cs].rearrange("(kc p) n -> p kc n", p=128), [], [f"wi{sl}"], dkey=f"wsb{sl}", eng="pool")
            for tt in range(NT):
                rs = slice(tt * 128, (tt + 1) * 128)
                g = it % 2; it += 1
                bg, be = 2 * g, 2 * g + 1
                xs_ = slice(self.own0 + tt * 128, self.own0 + (tt + 1) * 128)
                self.ld(xo[g], self.xres[xs_, cs], ["xres"], [f"xo{g}"], dkey=f"xo{g}")
                for kc in range(KC):
                    self.mm(self.psum[bg], hT[:, kc, rs], ws[sl][:, kc, :], kc == 0, kc == KC - 1, [f"ws{sl}", "hT"], [f"ps{bg}"])
                for kc in range(2):
                    self.mm(self.psum[be], pT[:, kc, rs], wi[sl][:, kc, :], kc == 0, kc == 1, [f"wi{sl}", "pT"], [f"ps{be}"])
                self.act(sg[g], self.psum[bg], AF.Sigmoid, [f"ps{bg}"], [f"sg{g}"])
                self.op("dve", "tensor_tensor", [f"sg{g}", f"ps{be}"], [f"sg{g}"], out=sg[g], in0=sg[g], in1=self.psum[be], op=ALU.mult)
                self.op("pool", "tensor_tensor", [f"sg{g}", f"xo{g}"], [f"xn{g}"], out=xn[g], in0=sg[g], in1=xo[g], op=ALU.add)
                self.ld(self.xres[xs_, cs], xn[g], [f"xn{g}"], ["xres"], dkey=f"xn{g}")
        P.barrier()

    def phase_z(self, l):
        P = self.P
        W = self.w["w_in"]
        a = self.Bump(self, OFF_A, OFF_B)
        self.cqT = a([128, 4, T], BF16)
        self.ckvT = a([128, 2, S], BF16)
        self.KrT2 = a([128, S], BF16)
        self.ld(self.gq_b, self.w["g_q_latent"][l].partition_broadcast(128), [], ["gq_b"], dkey="c_gq")
        self.ld(self.gkv_b, self.w["g_kv_latent"][l].partition_broadcast(128), [], ["gkv_b"], dkey="c_gkv")
        if self.skip_kv:
            self.ld(self.KrT2, self.KrT2_d, ["KrT2_d"], ["KrT2"], dkey="krs")
        for ps_ in range(1 if self.skip_kv else 2):
            tok0 = ps_ * T
            hT, _ = self.carve(OFF_B, [128, KC, T], BF16)
            b = self.Bump(self, OFF_C)
            tag = f"z{l}{ps_}"
            self.norm_pass(b, self.src, (self.own0 if ps_ == 0 else self.oth0), NT, self.w["g_mix"][l], hT, "hT", tag)
            if "dbg_hT" in self.dbg_o and ps_ == 0 and l == 0 and self.own0 == 0:
                self.out_ops.append(self.ld(self.dbg_o["dbg_hT"], hT, ["hT"], [], dkey="dbg"))
            ws = [b([128, KC, 512], BF16) for _ in range(2)]
            rb = [b([128, 512], BF16) for _ in range(2)]
            t1 = b([128, 8, 16], F32); t2 = b([128, 8, 16], F32)
            t1m = b([128, 1, 64], F32); t2m = b([128, 1, 64], F32)
            qts = [b([128, 512], BF16) for _ in range(2)]
            gst = [b([128, 512], F32) for _ in range(2)]
            cqf = b([128, 512], F32)
            cqs = b([128, 1], F32)
            cqj = b([128, 512], BF16)
            krb = b([128, 128], BF16)
            groups = []
            if ps_ == 0:
                groups += [("dq", C_DQ, 512), ("dq", C_DQ + 512, 512)]
            if not self.skip_kv:
                groups += [("dk", C_DK, 512), ("dk", C_DK + 512, 512), ("dv", C_DV, 512), ("dv", C_DV + 512, 512)]
            if ps_ == 0:
                groups += [("cq", C_CQ, 512)]
            if not self.skip_kv:
                groups += [("ckv", C_CKV, 320)]
            if ps_ == 0:
                groups += [("g", C_GA + i * 512, 512) for i in range(8)]
            cnt = {"ps": 0, "rb": 0, "qts": 0, "gst": 0}
            for gi, (kind, c0, ncol) in enumerate(groups):
                sl = gi % 2
                wsl = ws[sl][:, :, 0:ncol]
                self.ld(wsl, W[l, :, c0:c0 + ncol].rearrange("(kc p) n -> p kc n", p=128), [], [f"ws{sl}"], dkey=f"ws{sl}", eng="pool")
                if kind == "g":
                    gsel, chunk0 = divmod((c0 - C_GA) // 128, KC)
                    for cc in range(4):
                        for tb in range(T // 512):
                            bank = cnt["ps"] % 4; cnt["ps"] += 1
                            ps = self.psum[bank]
                            for kc in range(KC):
                                self.mm(ps, wsl[:, kc, cc * 128:(cc + 1) * 128], hT[:, kc, tb * 512:(tb + 1) * 512], kc == 0, kc == KC - 1,
                                        [f"ws{sl}", "hT"], [f"ps{bank}"])
                            gs = cnt["gst"] % 2; cnt["gst"] += 1
                            self.act(gst[gs], ps, AF.Sigmoid, [f"ps{bank}"], [f"gst{gs}"])
                            self.ld(self.GT_d[gsel, chunk0 + cc, :, tb * 512:(tb + 1) * 512], gst[gs], [f"gst{gs}"], ["GT_d"], dkey=f"gst{gs}")
                    continue
                for tt in range(NT):
                    gt = ps_ * NT + tt
                    tab = (self.own0 if ps_ == 0 else self.oth0) // 128 + tt
                    bank = cnt["ps"] % 4; cnt["ps"] += 1
                    ps = self.psum[bank][:, 0:ncol]
                    for kc in range(KC):
                        self.mm(ps, hT[:, kc, tt * 128:(tt + 1) * 128], wsl[:, kc, :], kc == 0, kc == KC - 1, [f"ws{sl}", "hT"], [f"ps{bank}"])
                    pk = [f"ps{bank}"]
                    if kind in ("dq", "dk"):
                        r = cnt["rb"] % 2; cnt["rb"] += 1
                        self.act(rb[r], ps, AF.Copy, pk, [f"rb{r}"])
                        ps3 = ps.rearrange("p (s d) -> p s d", d=64)
                        rb3 = rb[r].rearrange("p (s d) -> p s d", d=64)
                        self.rope((t1, t2), ps3, rb3, self.CCp[:, tab, :], self.SSp[:, tab, :], 16, 8, pk + ([f"rb{r}"] if os.environ.get("KSER") else []), [f"rb{r}"], "rp")
                        tbank = TB0 + cnt["qts"] % 2
                        q = cnt["qts"] % 2; cnt["qts"] += 1
                        pt = self.psum[tbank].bitcast(BF16)[:, 0:512].rearrange("p (a b) -> p a b", b=128)
                        for j in range(4):
                            self.tr(pt[:, j, :], rb[r][:, j * 128:(j + 1) * 128], [f"rb{r}"], [f"ps{tbank}"])
                        q3 = qts[q].rearrange("p (a b) -> p a b", b=128)
                        self.op("dve", "tensor_copy", [f"ps{tbank}"], [f"qts{q}"], out=q3, in_=pt)
                        h0 = ((c0 - (C_DQ if kind == "dq" else C_DK)) // 128)
                        if kind == "dq":
                            dst = self.QT_d[h0:h0 + 4, :, tt * 128:(tt + 1) * 128].rearrange("h p t -> p h t")
                            self.ld(dst, q3, [f"qts{q}"], ["QT_d"], dkey=f"qts{q}")
                        else:
                            dst = self.KT_d[h0:h0 + 4, :, gt * 128:(gt + 1) * 128].rearrange("h p t -> p h t")
                            self.ld(dst, q3, [f"qts{q}"], ["KT_d"], dkey=f"qts{q}")
                    elif kind == "dv":
                        r = cnt["rb"] % 2; cnt["rb"] += 1
                        self.act(rb[r], ps, AF.Copy, pk, [f"rb{r}"])
                        self.ld(self.V_d[gt * 128:(gt + 1) * 128, c0 - C_DV:c0 - C_DV + 512], rb[r], [f"rb{r}"], ["V_d"], dkey=f"rb{r}")
                    elif kind == "cq":
                        self.act(cqj, ps, AF.Square, pk, ["cqj", "cqs"], accum_out=cqs)
                        self.act(cqs, cqs, AF.Sqrt, ["cqs"], ["cqs"], bias=1e-6, scale=1.0 / 512)
                        self.op("dve", "reciprocal", ["cqs"], ["cqs"], out=cqs, in_=cqs)
                        r = cnt["rb"] % 2; cnt["rb"] += 1
                        self.op("dve", "scalar_tensor_tensor", pk + ["cqs", "gq_b"], [f"rb{r}"], out=rb[r], in0=ps, scalar=cqs[:, 0:1], in1=self.gq_b,
                                op0=ALU.mult, op1=ALU.mult)
                        tbank = TB0 + cnt["qts"] % 2; cnt["qts"] += 1
                        pt = self.psum[tbank].bitcast(BF16)[:, 0:512].rearrange("p (a b) -> p a b", b=128)
                        for j in range(4):
                            self.tr(pt[:, j, :], rb[r][:, j * 128:(j + 1) * 128], [f"rb{r}"], [f"ps{tbank}"])
                        self.op("dve", "tensor_copy", [f"ps{tbank}"], ["cqT"], out=self.cqT[:, :, tt * 128:(tt + 1) * 128], in_=pt)
                    elif kind == "ckv":
                        self.act(cqj[:, 0:256], ps[:, 0:256], AF.Square, pk, ["cqj", "cqs"], accum_out=cqs)
                        self.act(cqs, cqs, AF.Sqrt, ["cqs"], ["cqs"], bias=1e-6, scale=1.0 / 256)
                        self.op("dve", "reciprocal", ["cqs"], ["cqs"], out=cqs, in_=cqs)
                        r = cnt["rb"] % 2; cnt["rb"] += 1
                        self.op("dve", "scalar_tensor_tensor", pk + ["cqs", "gkv_b"], [f"rb{r}"], out=rb[r][:, 0:256], in0=ps[:, 0:256], scalar=cqs[:, 0:1],
                                in1=self.gkv_b, op0=ALU.mult, op1=ALU.mult)
                        ps3 = ps[:, 256:320].rearrange("p (s d) -> p s d", d=64)
                        k3 = krb[:, 0:64].rearrange("p (s d) -> p s d", d=64)
                        self.rope((t1m, t2m), ps3, k3, self.CCm[:, tab, :], self.SSm[:, tab, :], 64, 1, pk, ["krb"], "rm")
                        self.op("dve", "tensor_copy", ["krb"], ["krb"], out=krb[:, 64:128], in_=krb[:, 0:64])
                        tbank = TB0 + cnt["qts"] % 2; cnt["qts"] += 1
                        pt = self.psum[tbank].bitcast(BF16)[:, 0:384].rearrange("p (a b) -> p a b", b=128)
                        for j in range(2):
                            self.tr(pt[:, j, :], rb[r][:, j * 128:(j + 1) * 128], [f"rb{r}"], [f"ps{tbank}"])
                        self.tr(pt[:, 2, :], krb, ["krb"], [f"ps{tbank}"])
                        self.op("dve", "tensor_copy", [f"ps{tbank}"], ["ckvT"], out=self.ckvT[:, :, gt * 128:(gt + 1) * 128], in_=pt[:, 0:2, :])
                        self.act(self.KrT2[:, gt * 128:(gt + 1) * 128], pt[:, 2, :], AF.Copy, [f"ps{tbank}"], ["KrT2"])
            P.barrier()
        if not self.skip_kv:
            self.ld(self.KrT2_d, self.KrT2, ["KrT2"], ["KrT2_d"], dkey="krs")
        for n, t_, k_ in (("dbg_cqT", self.cqT, "cqT"), ("dbg_ckvT", self.ckvT, "ckvT"), ("dbg_KrT2", self.KrT2, "KrT2")):
            if n in self.dbg_o:
                self.out_ops.append(self.ld(self.dbg_o[n], t_, [k_], [], dkey="dbg"))

    def phase_final(self):
        b = self.Bump(self, OFF_A)
        g_b = b([128, D], F32)
        xt = [b([128, D], F32) for _ in range(2)]
        yt = [b([128, D], F32) for _ in range(2)]
        jk = b([128, D], BF16)
        ss = [b([128, 1], F32) for _ in range(2)]
        self.ld(g_b, self.w["g_final"].partition_broadcast(128), [], ["fg_b"], dkey="n_g")
        for tt in range(NT):
            g = tt % 2
            rs = slice(tt * 128, (tt + 1) * 128)
            self.ld(xt[g], self.xres[rs, :], ["xres"], [f"fxt{g}"], dkey=f"xo{g}")
            self.act(jk, xt[g], AF.Square, [f"fxt{g}"], ["fjk", f"fss{g}"], accum_out=ss[g])
            self.act(ss[g], ss[g], AF.Sqrt, [f"fss{g}"], [f"fss{g}"], bias=1e-6, scale=1.0 / D)
            self.op("dve", "reciprocal", [f"fss{g}"], [f"fss{g}"], out=ss[g], in_=ss[g])
            self.op("dve", "scalar_tensor_tensor", [f"fxt{g}", f"fss{g}", "fg_b"], [f"fyt{g}"], out=yt[g], in0=xt[g], scalar=ss[g][:, 0:1], in1=g_b,
                    op0=ALU.mult, op1=ALU.mult)
            self.out_ops.append(self.ld(self.out[rs, :], yt[g], [f"fyt{g}"], [], dkey=f"xn{g}"))

    def phase_copy_out(self):
        for i in range(4):
            rs = slice(i * 512, (i + 1) * 512)
            self.out_ops.append(self.ld(self.out[rs, :], self.xres[rs, :], ["xres"], [], dkey=f"co{i}"))


_WNAMES = ["g_mix", "w_in", "lambda_q1", "lambda_k1", "lambda_q2", "lambda_k2", "g_subln", "g_q_latent", "w_q_up", "g_kv_latent",
           "w_kv_up", "w_branch_diff", "w_branch_mla", "w_out", "g_ffn", "w_gate_up", "w_down", "w_ple_in", "g_ple", "w_ple_gate", "g_final"]


def core_inputs(c, x_full, p, positions, weights):
    b, hf = divmod(c, 2)
    own = slice(hf * T, (hf + 1) * T)
    oth = slice((1 - hf) * T, (2 - hf) * T)
    xin = np.ascontiguousarray(np.concatenate([x_full[b, own], x_full[b, oth]], axis=0))
    pos = np.concatenate([positions[b, own], positions[b, oth]], axis=0).astype(np.int32)
    pos = np.ascontiguousarray(pos.reshape(NS, 128).T)
    pin = np.ascontiguousarray(np.concatenate([p[:, b, own, :], p[:, b, oth, :]], axis=1))
    d = {"xin": xin, "pos": pos, "pin": pin}
    d.update(weights)
    return d


def kernel(**inputs):
    x = np.asarray(inputs["x"], dtype=np.float32)
    p = np.asarray(inputs["p"], dtype=np.float32)
    positions = np.asarray(inputs["positions"])
    weights = {n: np.ascontiguousarray(np.asarray(inputs[n], dtype=np.float32)) for n in _WNAMES}
    nc = Builder([(0, "xin", 0, T, False), (0, "xin", T, 0, True), (1, "xres", 0, T, False)], final_norm=True).build()
    in_maps = [core_inputs(c, x, p, positions, weights) for c in range(8)]
    res = run_bass_kernel_spmd(nc, in_maps, core_ids=list(range(8)))
    out = np.empty_like(x)
    for c in range(8):
        b, hf = divmod(c, 2)
        out[b, hf * T:(hf + 1) * T] = res.results[c]["out"]
    return out
```
